# Optimizing a Trainium2 kernel written in Bass

```python
import math
import jax, jax.numpy as jnp
from jax import lax
import numpy as np

D_MODEL = 1024
BATCH = 8
SEQ = 2048
DEPTH = 1
DEC_BATCH = 128
DEC_SEQ = 4
PAST_LEN = 16384
PAGE_SIZE = 128

M_HEADS = 16
M_HEAD_DIM = 64
M_INNER = M_HEADS * M_HEAD_DIM
M_GROUPS = 4
M_STATE = 128
M_CONV = 4
M_CONV_DIM = M_INNER + 2 * M_GROUPS * M_STATE
M_CHUNK = 64
M_DT_MIN = 0.001
M_DT_MAX = 0.1
H_HEADS = 8
H_KEY = 128
H_VAL = 128
H_QK = H_HEADS * H_KEY
H_INNER = H_HEADS * H_VAL
H_CHUNK = 32
D_FF = 2816
ALPHA = (2.0 * DEPTH) ** 0.25
BETA = (8.0 * DEPTH) ** -0.25
EPS = 1e-5
F32 = jnp.float32
_IN_SIZES = (M_INNER, M_CONV_DIM, M_HEADS, H_QK, H_QK, H_INNER, H_INNER, D_MODEL, D_MODEL)
IN_COLS = sum(_IN_SIZES)

kernel_name = 'hybrid_ssd_hgrn2_macaron_deepnorm_step'


def _split_points():
    pts, acc = [], 0
    for s in _IN_SIZES[:-1]:
        acc += s
        pts.append(acc)
    return pts


def _tm(t):
    return jnp.moveaxis(t, 1, 0)


def _layer_norm(x, g, b):
    xf = x.astype(F32)
    mu = jnp.mean(xf, -1, keepdims=True)
    var = jnp.mean(jnp.square(xf - mu), -1, keepdims=True)
    return ((xf - mu) * lax.rsqrt(var + EPS) * g.astype(F32) + b.astype(F32)).astype(x.dtype)


def _swiglu(x, wg, wu, wd):
    return (jax.nn.silu(x @ wg) * (x @ wu)) @ wd


def _causal_conv(u, buf, w, b):
    length = u.shape[1]
    up = jnp.concatenate([buf.astype(u.dtype), u], axis=1)
    out = b + sum(up[:, k:k + length] * w[k] for k in range(M_CONV))
    return jax.nn.silu(out), up[:, length:]


def _gated_rmsnorm(y, z, w):
    g = y * jax.nn.silu(z.astype(F32))
    gg = g.reshape(g.shape[:-1] + (M_GROUPS, M_INNER // M_GROUPS))
    gg = gg * lax.rsqrt(jnp.mean(gg * gg, -1, keepdims=True) + EPS)
    return gg.reshape(g.shape) * w.astype(F32)


def _ssd(xh, dt, a, b_in, c_in, h0):
    bsz, length = xh.shape[:2]
    csz = math.gcd(length, M_CHUNK)
    nc = length // csz
    rep = M_HEADS // M_GROUPS
    x = xh.reshape(bsz, nc, csz, M_GROUPS, rep, M_HEAD_DIM)
    dt = dt.reshape(bsz, nc, csz, M_GROUPS, rep)
    bm = b_in.reshape(bsz, nc, csz, M_GROUPS, M_STATE)
    cm = c_in.reshape(bsz, nc, csz, M_GROUPS, M_STATE)
    a_cum = jnp.cumsum(dt * a.reshape(M_GROUPS, rep), axis=2)
    causal = jnp.tril(jnp.ones((csz, csz), dtype=bool))[:, :, None, None]
    seg = a_cum[:, :, :, None] - a_cum[:, :, None, :]
    decay = jnp.exp(jnp.where(causal, seg, -jnp.inf))
    cb = jnp.einsum('bctgn,bcsgn->bctsg', cm, bm)
    w_ts = cb[..., None] * decay * dt[:, :, None]
    y_intra = jnp.einsum('bctsgr,bcsgrp->bctgrp', w_ts, x)
    a_last = a_cum[:, :, -1]
    to_end = jnp.exp(a_last[:, :, None] - a_cum) * dt
    chunk_states = jnp.einsum('bcsgn,bcsgrp->bcgrpn', bm, x * to_end[..., None])

    def step(h, inp):
        dec, st, c_c, acum_c = inp
        y_c = jnp.einsum('btgn,bgrpn->btgrp', c_c, h) * jnp.exp(acum_c)[..., None]
        return dec[..., None, None] * h + st, y_c

    h_init = h0.reshape(bsz, M_GROUPS, rep, M_HEAD_DIM, M_STATE)
    h_fin, y_inter = lax.scan(step, h_init, (_tm(jnp.exp(a_last)), _tm(chunk_states), _tm(cm), _tm(a_cum)))
    y = y_intra + _tm(y_inter)
    return y.reshape(bsz, length, M_HEADS, M_HEAD_DIM), h_fin.reshape(bsz, M_HEADS, M_HEAD_DIM, M_STATE)


def _hgrn2(q, log_f, k, v, s0):
    bsz, length = q.shape[:2]
    csz = math.gcd(length, H_CHUNK)
    nc = length // csz
    shp_k = (bsz, nc, csz, H_HEADS, H_KEY)
    q = q.reshape(shp_k)
    log_f = log_f.reshape(shp_k)
    k = k.reshape(shp_k)
    v = v.reshape(bsz, nc, csz, H_HEADS, H_VAL)
    b_cum = jnp.cumsum(log_f, axis=2)
    qd = q * jnp.exp(b_cum)
    kd = k * jnp.exp(-b_cum)
    causal = jnp.tril(jnp.ones((csz, csz), dtype=bool))
    scores = jnp.where(causal, jnp.einsum('bcthk,bcshk->bchts', qd, kd), 0.0)
    y_intra = jnp.einsum('bchts,bcshv->bcthv', scores, v)
    b_last = b_cum[:, :, -1]
    k_end = k * jnp.exp(b_last[:, :, None] - b_cum)

    def step(s, inp):
        dec, ke_c, v_c, qd_c = inp
        y_c = jnp.einsum('bthk,bhkv->bthv', qd_c, s)
        s_new = dec[..., None] * s + jnp.einsum('bshk,bshv->bhkv', ke_c, v_c)
        return s_new, y_c

    s_fin, y_inter = lax.scan(step, s0, (_tm(jnp.exp(b_last)), _tm(k_end), _tm(v), _tm(qd)))
    y = y_intra + _tm(y_inter)
    return y.reshape(bsz, length, H_HEADS, H_VAL), s_fin


def _layer(x, conv_buf, ssm_h, hgrn_s, lb, w):
    (f1_wg, f1_wu, f1_wd, ln1_g, ln1_b, w_in, conv_w, conv_b, dt_bias, a_log, d_skip, m_norm_w, w_m_out,
     h_norm_w, w_h_out, w_o, ln2_g, ln2_b, f2_wg, f2_wu, f2_wd, ln3_g, ln3_b) = w
    dtype = x.dtype
    bsz, length, _ = x.shape
    x = _layer_norm(ALPHA * x + 0.5 * _swiglu(x, f1_wg, f1_wu, f1_wd), ln1_g, ln1_b)
    proj = x @ w_in
    z, xbc, dt_raw, q, f_raw, i_in, og, gate_m, gate_h = jnp.split(proj, _split_points(), axis=-1)
    xbc, new_conv = _causal_conv(xbc, conv_buf, conv_w, conv_b)
    xm, b_in, c_in = jnp.split(xbc.astype(F32), [M_INNER, M_INNER + M_GROUPS * M_STATE], axis=-1)
    xm = xm.reshape(bsz, length, M_HEADS, M_HEAD_DIM)
    dt = jax.nn.softplus(dt_raw.astype(F32) + dt_bias.astype(F32))
    a = -jnp.exp(a_log.astype(F32))
    y_ssd, new_h = _ssd(xm, dt, a, b_in.reshape(bsz, length, M_GROUPS, M_STATE),
                        c_in.reshape(bsz, length, M_GROUPS, M_STATE), ssm_h.astype(F32))
    y_m = y_ssd + d_skip.astype(F32)[:, None] * xm
    y_m = _gated_rmsnorm(y_m.reshape(bsz, length, M_INNER), z, m_norm_w).astype(dtype) @ w_m_out
    fr = f_raw.astype(F32)
    log_f = jnp.log(lb + (1.0 - lb) * jax.nn.sigmoid(fr))
    k = (1.0 - lb) * jax.nn.sigmoid(-fr)
    o, new_s = _hgrn2(q.astype(F32).reshape(bsz, length, H_HEADS, H_KEY),
                      log_f.reshape(bsz, length, H_HEADS, H_KEY),
                      k.reshape(bsz, length, H_HEADS, H_KEY),
                      i_in.astype(F32).reshape(bsz, length, H_HEADS, H_VAL), hgrn_s.astype(F32))
    o = o * lax.rsqrt(jnp.mean(o * o, -1, keepdims=True) + EPS)
    o = o.reshape(bsz, length, H_INNER) * h_norm_w.astype(F32) * jax.nn.silu(og.astype(F32))
    y_h = o.astype(dtype) @ w_h_out
    mix = jax.nn.sigmoid(gate_m) * y_m + jax.nn.sigmoid(gate_h) * y_h
    x = _layer_norm(ALPHA * x + mix @ w_o, ln2_g, ln2_b)
    x = _layer_norm(ALPHA * x + 0.5 * _swiglu(x, f2_wg, f2_wu, f2_wd), ln3_g, ln3_b)
    return x, new_conv.astype(conv_buf.dtype), new_h.astype(ssm_h.dtype), new_s.astype(hgrn_s.dtype)


def setup_inputs(seed: int = 0) -> dict:
    key = jax.random.key(seed)
    ks = iter(jax.random.split(key, 48))

    def nrm(shape, scale):
        return scale * jax.random.normal(next(ks), shape, F32)

    L = DEPTH
    dt0 = jnp.exp(jax.random.uniform(next(ks), (L, M_HEADS), F32, math.log(M_DT_MIN), math.log(M_DT_MAX)))
    dt_bias = dt0 + jnp.log(-jnp.expm1(-dt0))
    a_log = jnp.log(jax.random.uniform(next(ks), (L, M_HEADS), F32, 1.0, 16.0))
    return {
        'x_prompt': nrm((BATCH, SEQ, D_MODEL), 1.0),
        'x_sample': nrm((DEC_BATCH, DEC_SEQ, D_MODEL), 1.0),
        'state_conv': nrm((L, DEC_BATCH, M_CONV - 1, M_CONV_DIM), 1.0),
        'state_ssm': nrm((L, DEC_BATCH, M_HEADS, M_HEAD_DIM, M_STATE), 0.3),
        'state_hgrn': nrm((L, DEC_BATCH, H_HEADS, H_KEY, H_VAL), 0.3),
        'ffn1_w_gate': nrm((L, D_MODEL, D_FF), D_MODEL ** -0.5),
        'ffn1_w_up': nrm((L, D_MODEL, D_FF), D_MODEL ** -0.5),
        'ffn1_w_down': nrm((L, D_FF, D_MODEL), BETA * D_FF ** -0.5),
        'ln1_g': 1.0 + nrm((L, D_MODEL), 0.02),
        'ln1_b': nrm((L, D_MODEL), 0.01),
        'w_in': nrm((L, D_MODEL, IN_COLS), D_MODEL ** -0.5),
        'conv_w': nrm((L, M_CONV, M_CONV_DIM), M_CONV ** -0.5),
        'conv_b': nrm((L, M_CONV_DIM), 0.01),
        'dt_bias': dt_bias,
        'a_log': a_log,
        'd_skip': 1.0 + nrm((L, M_HEADS), 0.1),
        'm_norm_w': 1.0 + nrm((L, M_INNER), 0.02),
        'w_m_out': nrm((L, M_INNER, D_MODEL), BETA * M_INNER ** -0.5),
        'hgrn_lb_param': nrm((L + 1, H_QK), 0.1),
        'h_norm_w': 1.0 + nrm((L, H_INNER), 0.02),
        'w_h_out': nrm((L, H_INNER, D_MODEL), BETA * H_INNER ** -0.5),
        'w_o': nrm((L, D_MODEL, D_MODEL), BETA * D_MODEL ** -0.5),
        'ln2_g': 1.0 + nrm((L, D_MODEL), 0.02),
        'ln2_b': nrm((L, D_MODEL), 0.01),
        'ffn2_w_gate': nrm((L, D_MODEL, D_FF), D_MODEL ** -0.5),
        'ffn2_w_up': nrm((L, D_MODEL, D_FF), D_MODEL ** -0.5),
        'ffn2_w_down': nrm((L, D_FF, D_MODEL), BETA * D_FF ** -0.5),
        'ln3_g': 1.0 + nrm((L, D_MODEL), 0.02),
        'ln3_b': nrm((L, D_MODEL), 0.01),
    }


def reference(x_prompt, x_sample, state_conv, state_ssm, state_hgrn,
              ffn1_w_gate, ffn1_w_up, ffn1_w_down, ln1_g, ln1_b, w_in, conv_w, conv_b, dt_bias, a_log,
              d_skip, m_norm_w, w_m_out, hgrn_lb_param, h_norm_w, w_h_out, w_o, ln2_g, ln2_b,
              ffn2_w_gate, ffn2_w_up, ffn2_w_down, ln3_g, ln3_b):
    lb_all = jnp.cumsum(jax.nn.softmax(hgrn_lb_param.astype(F32), axis=0), axis=0)
    bp = x_prompt.shape[0]
    xp, xs = x_prompt, x_sample
    conv_p, ssm_p, hg_p, conv_s, ssm_s, hg_s = [], [], [], [], [], []
    for l in range(DEPTH):
        w = (ffn1_w_gate[l], ffn1_w_up[l], ffn1_w_down[l], ln1_g[l], ln1_b[l], w_in[l], conv_w[l], conv_b[l],
             dt_bias[l], a_log[l], d_skip[l], m_norm_w[l], w_m_out[l], h_norm_w[l], w_h_out[l], w_o[l],
             ln2_g[l], ln2_b[l], ffn2_w_gate[l], ffn2_w_up[l], ffn2_w_down[l], ln3_g[l], ln3_b[l])
        lb = lb_all[l]
        zc = jnp.zeros((bp, M_CONV - 1, M_CONV_DIM), state_conv.dtype)
        zh = jnp.zeros((bp, M_HEADS, M_HEAD_DIM, M_STATE), state_ssm.dtype)
        zs = jnp.zeros((bp, H_HEADS, H_KEY, H_VAL), state_hgrn.dtype)
        xp, cp, sp, hp = _layer(xp, zc, zh, zs, lb, w)
        xs, cs, ss, hs = _layer(xs, state_conv[l], state_ssm[l], state_hgrn[l], lb, w)
        conv_p.append(cp); ssm_p.append(sp); hg_p.append(hp)
        conv_s.append(cs); ssm_s.append(ss); hg_s.append(hs)
    return (xp, xs, jnp.stack(conv_p), jnp.stack(ssm_p), jnp.stack(hg_p),
            jnp.stack(conv_s), jnp.stack(ssm_s), jnp.stack(hg_s))
```

```python
import numpy as np
import concourse.bass as bass
import concourse.mybir as mybir
from concourse.bass_utils import run_bass_kernel_spmd

F32 = mybir.dt.float32
BF16 = mybir.dt.bfloat16
AF = mybir.ActivationFunctionType
ALU = mybir.AluOpType

NCORES = 8
D = 1024
DFF = 2816
NFF = 22
KC = 8
SEQ = 2048
T = 256
NS = T // 128
NTILE = SEQ // T
ALPHA = float(2.0 ** 0.25)
EPS = 1e-5
GR = 64
NSLOT = 8
SLOT = 2048


def _esz(dt):
    return 2 if dt == BF16 else 4


class Sched:
    ENGS = ("pe", "act", "dve", "pool", "sp")

    def __init__(self, nc):
        self.nc = nc
        self.ops = {e: [] for e in self.ENGS}
        self.cnt = {e: 0 for e in self.ENGS}
        self.last_w = {}
        self.readers = {}
        self.waited = {e: {} for e in self.ENGS}
        self.dma_cnt = {}
        self.gcache = {}

    def gran(self, ap):
        key = (ap.tensor.name, ap.offset, tuple(ap.ap), str(ap.dtype))
        g = self.gcache.get(key)
        if g is not None:
            return g
        if ap.tensor.name.startswith("pa") or ap.tensor.name.startswith("pt"):
            g = frozenset([(ap.tensor.name, "bank")])
            self.gcache[key] = g
            return g
        dims = list(ap.ap)
        pstride = dims[0][0]
        off = ap.offset % pstride if pstride else ap.offset
        free = [(s, c) for (s, c) in dims[1:] if c > 1 or len(dims) == 2]
        esz = _esz(ap.dtype)
        name = ap.tensor.name
        out = set()
        if not free:
            free = [(1, 1)]
        inner_s, inner_c = free[-1]
        outer = free[:-1]
        span = (inner_c - 1) * abs(inner_s) + 1

        def rec(i, base):
            if i == len(outer):
                lo = base * esz
                hi = (base + span) * esz
                for gidx in range(lo // GR, (hi - 1) // GR + 1):
                    out.add((name, gidx))
                return
            s, c = outer[i]
            for j in range(c):
                rec(i + 1, base + j * s)
                if s == 0:
                    break

        rec(0, off)
        g = frozenset(out)
        self.gcache[key] = g
        return g

    def op(self, eng, fn, outs=(), ins=(), dma=None, extra=()):
        deps = {}

        def add(tok):
            if tok is None:
                return
            k = (tok[0], tok[1])
            if tok[2] > deps.get(k, 0):
                deps[k] = tok[2]

        for tk_ in extra:
            add(tk_)
        og = set()
        for ap in outs:
            og |= self.gran(ap)
        ig = set()
        for ap in ins:
            ig |= self.gran(ap)
        for g in list(ig):
            if g[1] == "bank":
                ig.discard(g)
                og.add(g)
        for g in ig:
            add(self.last_w.get(g))
        for g in og:
            add(self.last_w.get(g))
            r = self.readers.get(g)
            if r:
                for k, v in r.items():
                    if v > deps.get(k, 0):
                        deps[k] = v
        if dma is not None:
            n = self.dma_cnt.get(dma, 0) + 1
            self.dma_cnt[dma] = n
            tok = ("dma", dma, n)
        else:
            n = self.cnt[eng] + 1
            self.cnt[eng] = n
            tok = ("eng", eng, n)
        waits = []
        wd = self.waited[eng]
        for k, v in deps.items():
            if k[0] == "eng" and k[1] == "pe" and eng == "pe":
                continue
            if k[0] == "eng" and k[1] == eng and eng in ("dve", "act") and dma is None and v < self.cnt[eng] - 1:
                continue
            if v > wd.get(k, 0):
                wd[k] = v
                waits.append((k, v))
        self.ops[eng].append((waits, fn, tok))
        for g in og:
            self.last_w[g] = tok
            self.readers[g] = {}
        tk = (tok[0], tok[1])
        for g in ig:
            if g in og:
                continue
            r = self.readers.setdefault(g, {})
            if tok[2] > r.get(tk, 0):
                r[tk] = tok[2]
        return tok

    def mm(self, out, pairs, start=True, stop=True, skip=False):
        def fn(e):
            n = len(pairs)
            ins = None
            for i, (l, r) in enumerate(pairs):
                if skip:
                    ins = e.matmul(out, lhsT=l, rhs=r, start=(start and i == 0), stop=(stop and i == n - 1),
                                   skip_group_check=True)
                else:
                    ins = e.matmul(out, lhsT=l, rhs=r, start=(start and i == 0), stop=(stop and i == n - 1))
            return ins
        self.op("pe", fn, outs=[out], ins=[a for p in pairs for a in p])

    def tr(self, out, in_, ident):
        self.op("pe", lambda e: e.transpose(out, in_, ident), outs=[out], ins=[in_, ident])

    def actf(self, out, in_, func, bias=None, scale=None, eng="act"):
        kw = {}
        ins = [in_]
        if bias is not None:
            kw["bias"] = bias
            if not isinstance(bias, float):
                ins.append(bias)
        if scale is not None:
            kw["scale"] = scale
            if not isinstance(scale, float):
                ins.append(scale)
        self.op("act", lambda e: e.activation(out=out, in_=in_, func=func, **kw), outs=[out], ins=ins)

    def tt(self, out, in0, in1, op, eng="dve"):
        self.op(eng, lambda e: e.tensor_tensor(out=out, in0=in0, in1=in1, op=op), outs=[out], ins=[in0, in1])

    def ts(self, out, in0, s1, s2, op0, op1=None, eng="dve"):
        ins = [in0]
        if not isinstance(s1, float):
            ins.append(s1)
        if s2 is not None and not isinstance(s2, float):
            ins.append(s2)
        if op1 is None:
            self.op(eng, lambda e: e.tensor_scalar(out=out, in0=in0, scalar1=s1, scalar2=None, op0=op0), outs=[out], ins=ins)
        else:
            self.op(eng, lambda e: e.tensor_scalar(out=out, in0=in0, scalar1=s1, scalar2=s2, op0=op0, op1=op1), outs=[out], ins=ins)

    def stt(self, out, in0, scalar, in1, op0, op1, eng="dve"):
        ins = [in0, in1]
        if not isinstance(scalar, float):
            ins.append(scalar)
        self.op(eng, lambda e: e.scalar_tensor_tensor(out=out, in0=in0, scalar=scalar, in1=in1, op0=op0, op1=op1), outs=[out], ins=ins)

    def copy(self, out, in_, eng="dve"):
        if eng == "act":
            self.op("act", lambda e: e.activation(out=out, in_=in_, func=AF.Copy), outs=[out], ins=[in_])
        else:
            self.op(eng, lambda e: e.tensor_copy(out=out, in_=in_), outs=[out], ins=[in_])

    def memset(self, ap, val, eng="dve"):
        self.op(eng, lambda e: e.memset(ap, val), outs=[ap])

    def dma(self, out, in_, key, eng="sp", sb_out=True):
        if sb_out:
            self.op(eng, lambda e: e.dma_start(out=out, in_=in_), outs=[out], dma=key)
        else:
            self.op(eng, lambda e: e.dma_start(out=out, in_=in_), ins=[in_], dma=key)

    def emit(self):
        nc = self.nc
        sems = {}
        for e in self.ENGS:
            sems[("eng", e)] = nc.alloc_semaphore("s_" + e)
        for i, k in enumerate(self.dma_cnt):
            sems[("dma", k)] = nc.alloc_semaphore("d%d" % i)
        final = [(("dma", k), v) for k, v in self.dma_cnt.items()]

        def run(ename, eobj):
            for waits, fn, tok in self.ops[ename]:
                for k, v in waits:
                    eobj.wait_ge(sems[k], v * (16 if k[0] == "dma" else 1))
                ins = fn(eobj)
                ins.then_inc(sems[(tok[0], tok[1])], 16 if tok[0] == "dma" else 1)
            if ename == "sp":
                for k, v in final:
                    eobj.wait_ge(sems[k], v * 16)
                for e2 in self.ENGS:
                    if e2 != "sp" and self.cnt[e2] > 0:
                        eobj.wait_ge(sems[("eng", e2)], self.cnt[e2])

        with nc.Block() as block:
            @block.tensor
            def _(e):
                run("pe", e)

            @block.scalar
            def _(e):
                run("act", e)

            @block.vector
            def _(e):
                run("dve", e)

            @block.gpsimd
            def _(e):
                run("pool", e)

            @block.sync
            def _(e):
                run("sp", e)


class Builder:
    def __init__(self, mode="full"):
        self.mode = mode
        self.nc = bass.Bass("TRN2", target_bir_lowering=False)
        self.k = Sched(self.nc)
        self.wi = 0
        self.rings = {"M": [0, 1, 2, 3], "F": [4, 5, 6, 7]}
        self.wcnt = {"M": 0, "F": 0}
        self.conv_tok = {}
        self.cv_hist = []
        self.ncv = 0

    def din(self, name, shape, dt=F32):
        return self.nc.dram_tensor(name, list(shape), dt, kind="ExternalInput").ap()

    def dout(self, name, shape, dt=F32):
        return self.nc.dram_tensor(name, list(shape), dt, kind="ExternalOutput").ap()

    def sb(self, name, shape, dt=F32):
        return self.nc.alloc_sbuf_tensor(name, list(shape), dt)

    def ps(self, name, shape, dt=F32):
        return self.nc.alloc_psum_tensor(name, list(shape), dt)

    def wload(self, w, sel, n, pool="M"):
        k = self.k
        src, scr, name = w
        ukey = (name, self.unit_id(sel))
        tok = self.conv_tok.get(ukey)
        if tok is None:
            i = self.ncv
            self.ncv += 1
            prev = self.cv_hist[i - 4] if i >= 4 else None
            o_, i_ = sel(scr), sel(src)
            tok = k.op("pool", lambda e: e.dma_start(out=o_, in_=i_), dma=("cv", i % 4), extra=[prev] if prev else [])
            self.cv_hist.append(tok)
            self.conv_tok[ukey] = tok
        ring = self.rings[pool]
        sl = ring[self.wcnt[pool] % len(ring)]
        self.wcnt[pool] += 1
        dst = self.wslots[sl][:, 0:n]
        in_ = sel(scr)
        k.op("sp" if pool == "F" else "pool", lambda e: e.dma_start(out=dst, in_=in_), outs=[dst], dma=("w", sl), extra=[tok])
        return self.wslots[sl]

    def unit_id(self, sel):
        return sel.uid

    def build(self):
        nc, k = self.nc, self.k
        mode = self.mode
        xp = self.din("xp", [SEQ, D])
        xs_d = self.din("xs", [128, D])
        sconv_d = self.din("sconv", [128, 2048])
        sssm_d = self.din("sssm", [16, 1024, 128])
        shg_d = self.din("shg", [16, 1024, 128])
        def wpair(name, shape):
            return (self.din(name, shape), self.nc.dram_tensor(name + "_b16", list(shape), BF16, kind="Internal").ap(), name)

        class Sel:
            def __init__(se, uid, f):
                se.uid, se.f = uid, f

            def __call__(se, a):
                return se.f(a)

        wgu = [wpair("wgu1", [22, 128, 2048]), wpair("wgu2", [22, 128, 2048])]
        wd = [wpair("wd1", [2, 128, NFF * 512]), wpair("wd2", [2, 128, NFF * 512])]
        lnp = self.din("lnp", [6, 128, D])
        wtok = wpair("wtok", [4, 128, 4096])
        wdt_d = self.din("wdt", [128, 128])
        winf = wpair("winf", [28, 128, 2048])
        wmo = wpair("wmo", [4, 128, 2048])
        who = wpair("who", [4, 128, 2048])
        wo = wpair("wo", [2, 128, 4096])
        smallp = self.din("smallp", [128, 160])
        y_p = self.dout("y_p", [SEQ, D])
        conv_p = self.dout("conv_p", [3, 2048])
        ssm_p = self.dout("ssm_p", [1024, 128])
        hg_p = self.dout("hg_p", [1024, 128])
        y_s = self.dout("y_s", [64, D])
        conv_s = self.dout("conv_s", [48, 2048])
        ssm_s = self.dout("ssm_s", [16, 1024, 128])
        hg_s = self.dout("hg_s", [16, 1024, 128])

        self.wslots = [self.sb("wslot%d" % i, [128, SLOT], BF16) for i in range(NSLOT)]
        xtokS = [self.sb("xtok%d" % i, [128, NS, D]) for i in range(2)]
        xTS = [self.sb("xT%d" % i, [128, KC, T], BF16) for i in range(2)]
        hT = self.sb("hT", [128, NFF, T], BF16)
        sg = [self.sb("sg%d" % i, [128, T]) for i in range(2)]
        lnF = [self.sb("lnF%d" % i, [128, D]) for i in range(2)]
        lnM = [self.sb("lnM%d" % i, [128, D]) for i in range(2)]
        lnst = {}
        for pl in ("F", "M"):
            lnst[pl] = [(self.sb("st6%s%d" % (pl, i), [128, 2, 6]), self.sb("mv%s%d" % (pl, i), [128, 2]),
                         self.sb("rstd%s%d" % (pl, i), [128, 1]), self.sb("nmr%s%d" % (pl, i), [128, 1])) for i in range(2)]
        ident_f = self.sb("ident_f", [128, 128])
        ident_b = self.sb("ident_b", [128, 128], BF16)
        ones_f = self.sb("ones_f", [128, 128])
        tri_f = self.sb("tri_f", [128, 128])
        m1_f = self.sb("m1_f", [128, 128])
        bmask = self.sb("bmask", [128, 128])
        triS_f = self.sb("triS_f", [128, 128])
        cmk = self.sb("cmk", [128, 4])
        resetm = self.sb("resetm", [128, T])
        resetmS = self.sb("resetmS", [128, 128])
        tri_b = self.sb("tri_b", [128, 128], BF16)
        m1_b = self.sb("m1_b", [128, 128], BF16)
        ones_b = self.sb("ones_b", [128, 128], BF16)
        zero_b = self.sb("zero_b", [128, 128], BF16)
        triS_b = self.sb("triS_b", [128, 128], BF16)
        m1S_b = self.sb("m1S_b", [128, 128], BF16)
        onesS_b = self.sb("onesS_b", [128, 128], BF16)
        sel16 = self.sb("sel16", [128, 16], BF16)
        seqcol = self.sb("seqcol", [128, 16])
        seqrow = self.sb("seqrow", [128, 16, 128], BF16)
        sp_ = self.sb("smallp_sb", [128, 160])
        wdt = self.sb("wdt_sb", [128, KC, 16], BF16)
        a_bc = self.sb("a_bc", [128, 16])
        cwh_t = self.sb("cwh", [128, 64])
        cbh = self.sb("cbh", [128, 16])
        hnwh = self.sb("hnwh", [128, 8])
        omlh = self.sb("omlh", [128, 8])
        lbh = self.sb("lbh", [128, 8])
        lb = self.sb("lb", [128, 8])
        oml = self.sb("oml", [128, 8])
        cw = None
        cb = sp_[:, 64:80]
        dtb = sp_[:, 80:96]
        alog = sp_[:, 96:112]
        dsk = sp_[:, 112:128]
        mnw = sp_[:, 128:136]
        hnw = sp_[:, 136:144]
        lbp = sp_[:, 144:160].rearrange("p (h r) -> p h r", r=2)
        xbcT = self.sb("xbcT", [128, 16, T], BF16)
        pre = [self.sb("pre%d" % i, [128, T + 3]) for i in range(2)]
        cacc = [self.sb("cacc%d" % i, [128, T]) for i in range(2)]
        carry = self.sb("carry", [128, 16, 3])
        zs = self.sb("zs", [128, NS, D], BF16)
        vtok = self.sb("vtok", [128, NS, D], BF16)
        ar1 = self.sb("ar1", [128, 4096], BF16)
        rseg_hi = ar1[:, 0:1024]
        rseg_lo = ar1[:, 1024:2048]
        WT = ar1[:, 2048:4096].rearrange("p (h t) -> p h t", h=16)
        vm = ar1[:, :].rearrange("p (c d) -> p c d", c=4)
        ar2 = self.sb("ar2", [128, 4096], BF16)
        xs_ = ar2[:, 0:1024]
        xD = ar2[:, 1024:2048]
        xe = ar2[:, 2048:3072]
        xem = ar2[:, 3072:4096]
        scT = ar2[:, 0:1024].rearrange("p (h t) -> p h t", h=8)
        ketok = ar2[:, 1024:2048].rearrange("p (h t) -> p h t", h=8)
        osqb = ar2[:, 2048:2560]
        ar3 = self.sb("ar3", [128, 3072])
        yi = ar3[:, 0:1024]
        ysum = ar3[:, 1024:2048]
        eseg = ar3[:, 2048:3072]
        A = dict((n, ar3[:, i * T:(i + 1) * T]) for i, n in enumerate(("sig", "f", "logf", "b", "e1", "e2", "e3")))
        osq = ar3[:, 2048:3072]
        rs_ = osq
        t1 = yi
        outst = osq
        dtr = self.sb("dtr", [128, 16])
        adt = self.sb("adt", [128, 16])
        acs = self.sb("acs", [128, 16])
        ea = self.sb("ea", [128, 16])
        te = self.sb("te", [128, 16])
        dtte = self.sb("dtte", [128, 16])
        decbc = self.sb("decbc", [128, 16])
        adt_hi = self.sb("adt_hi", [128, 16], BF16)
        adt_lo = self.sb("adt_lo", [128, 16], BF16)
        adt_r = self.sb("adt_r", [128, 16])
        cmT = self.sb("cmT", [128, 4, 128], BF16)
        deccol = self.sb("deccol", [128, 8, 16])
        Btok = self.sb("Btok", [128, 4, 128], BF16)
        CBm = self.sb("CBm", [128, 4, 128])
        ssq = self.sb("ssq", [128, 4])
        gn = self.sb("gn", [128, D], BF16)
        hstT = self.sb("hstT", [128, D])
        hstT_b = self.sb("hstT_b", [128, D], BF16)
        Sst = self.sb("Sst", [128, 8, 128])
        Sst_b = self.sb("Sst_b", [128, 8, 128], BF16)
        qdT = self.sb("qdT", [128, 8, T], BF16)
        kdT = self.sb("kdT", [128, 8, T], BF16)
        keT = self.sb("keT", [128, 8, T], BF16)
        sog = self.sb("sog", [128, 8, T], BF16)
        decT = self.sb("decT", [128, 8, 32])
        sgm = self.sb("sgm", [128, 8, T], BF16)
        sgh = self.sb("sgh", [128, 8, T], BF16)
        gatedT = self.sb("gatedT", [128, 8, T], BF16)
        ofinT = self.sb("ofinT", [128, 8, T], BF16)
        mixT = self.sb("mixT", [128, 8, T], BF16)
        tmpa = self.sb("tmpa", [128, T])
        tmpb = self.sb("tmpb", [128, T])
        PA = [self.ps("pa%d" % i, [128, 512]) for i in range(6)]
        PT = [self.ps("pt%d" % i, [128, 1024], BF16) for i in range(2)]
        rot = {"M": 0, "F": 0}
        self.pt_i = 0

        def pbank(cx):
            pl = cx["pool"]
            b = PA[(0 if pl == "M" else 3) + rot[pl] % 3]
            rot[pl] += 1
            return b

        def ptbank():
            b = PT[self.pt_i % 2]
            self.pt_i += 1
            return b

        def asel(out, in_, pattern, op, base, cm):
            k.op("pool", lambda e: e.affine_select(out=out, in_=in_, pattern=pattern, compare_op=op, fill=0.0,
                                                    base=base, channel_multiplier=cm), outs=[out], ins=[in_])

        k.memset(ones_f[:], 1.0, eng="pool")
        asel(ident_f[:], ones_f[:], [[-1, 128]], ALU.is_equal, 0, 1)
        asel(tri_f[:], ones_f[:], [[1, 128]], ALU.is_ge, 0, -1)
        asel(m1_f[:], ones_f[:], [[-1, 128]], ALU.is_gt, 0, 1)
        k.copy(ident_b[:], ident_f[:], eng="pool")
        k.copy(tri_b[:], tri_f[:], eng="pool")
        k.copy(m1_b[:], m1_f[:], eng="pool")
        k.copy(ones_b[:], ones_f[:], eng="pool")
        k.memset(zero_b[:], 0.0, eng="pool")
        k.copy(bmask[:], tri_f[:], eng="pool")
        for c in range(1, 4):
            asel(bmask[:, 32 * c:32 * c + 32], bmask[:, 32 * c:32 * c + 32], [[0, 32]], ALU.is_ge, -32 * c, 1)
        for c in range(4):
            asel(cmk[:, c:c + 1], ones_f[:, 0:1], [[0, 1]], ALU.is_ge, -32 * c, 1)
            asel(cmk[:, c:c + 1], cmk[:, c:c + 1], [[0, 1]], ALU.is_gt, 32 * c + 32, -1)
        k.memset(resetm[:], 0.0, eng="pool")
        k.memset(resetm[:].rearrange("p (c j) -> p c j", j=32)[:, :, 0:1], 1.0, eng="pool")
        k.memset(resetmS[:], 0.0, eng="pool")
        k.memset(resetmS[:].rearrange("p (c j) -> p c j", j=4)[:, :, 0:1], 1.0, eng="pool")
        k.copy(triS_f[:], tri_f[:], eng="pool")
        k.copy(eseg[:, 0:128], m1_f[:], eng="pool")
        for c in range(32):
            if c > 0:
                asel(triS_f[:, 4 * c:4 * c + 4], triS_f[:, 4 * c:4 * c + 4], [[0, 4]], ALU.is_ge, -4 * c, 1)
            asel(eseg[:, 4 * c:4 * c + 4], eseg[:, 4 * c:4 * c + 4], [[0, 4]], ALU.is_gt, 4 * c + 4, -1)
        k.copy(triS_b[:], triS_f[:], eng="pool")
        k.copy(m1S_b[:], eseg[:, 0:128], eng="pool")
        k.tt(eseg[:, 0:128], eseg[:, 0:128], triS_f[:], ALU.add, eng="pool")
        k.copy(onesS_b[:], eseg[:, 0:128], eng="pool")
        asel(eseg[:, 128:144], ones_f[:, 0:16], [[-4, 16]], ALU.is_equal, -3, 1)
        k.copy(sel16[:], eseg[:, 128:144], eng="pool")
        asel(seqcol[:], ones_f[:, 0:16], [[-4, 16]], ALU.is_ge, 0, 1)
        asel(seqcol[:], seqcol[:], [[4, 16]], ALU.is_ge, 3, -1)
        k.memset(seqrow[:], 1.0, eng="pool")
        asel(seqrow[:], seqrow[:], [[-4, 16], [1, 128]], ALU.is_ge, 0, 0)
        asel(seqrow[:], seqrow[:], [[4, 16], [-1, 128]], ALU.is_ge, 3, 0)
        k.dma(sp_[:], smallp, key="c0")
        k.dma(wdt[:].rearrange("p k j -> p (k j)"), wdt_d, key="c1", eng="pool")
        k.actf(a_bc[:], alog, AF.Exp)
        k.ts(a_bc[:], a_bc[:], -1.0, None, ALU.mult)
        k.tt(lb[:], lbp[:, :, 0], lbp[:, :, 1], ALU.subtract)
        k.actf(lb[:], lb[:], AF.Tanh, scale=0.5)
        k.ts(oml[:], lb[:], -0.5, 0.5, ALU.mult, ALU.add)
        k.ts(lb[:], lb[:], 0.5, 0.5, ALU.mult, ALU.add)
        k.ts(omlh[:], oml[:], 0.5, None, ALU.mult)
        k.stt(lbh[:], oml[:], 0.5, lb[:], ALU.mult, ALU.add)
        k.ts(cwh_t[:], sp_[:, 0:64], 0.5, None, ALU.mult)
        k.ts(cbh[:], cb, 0.5, None, ALU.mult)
        k.ts(hnwh[:], hnw, 0.5, None, ALU.mult)
        cw = cwh_t[:].rearrange("p (c k) -> p c k", k=4)
        k.memset(carry[:], 0.0)
        k.memset(hstT[:], 0.0)
        k.memset(hstT_b[:], 0.0)
        k.memset(Sst[:], 0.0)
        k.memset(Sst_b[:], 0.0)

        def load_x(cx):
            xtok = xtokS[cx["st"]]
            if cx["smp"]:
                k.dma(xtok[:, 0, :], xs_d, key=("x", cx["st"], 0))
                return
            for s in range(NS):
                r0 = cx["ti"] * T + s * 128
                k.dma(xtok[:, s, :], xp[r0:r0 + 128, :], key=("x", cx["st"], s))

        def make_xT(cx):
            xtok, xT = xtokS[cx["st"]], xTS[cx["st"]]
            for s in range(cx["ns"]):
                for half in range(2):
                    pb = pbank(cx)
                    for j in range(4):
                        kc = half * 4 + j
                        k.tr(pb[:, j * 128:(j + 1) * 128], xtok[:, s, kc * 128:(kc + 1) * 128], ident_f[:])
                    k.copy(xT[:, half * 4:half * 4 + 4, s * 128:(s + 1) * 128],
                           pb[:].rearrange("p (j t) -> p j t", j=4), eng="act")
                    yield

        def ffn(cx, fi):
            tw, ns = cx["tw"], cx["ns"]
            xtok, xT = xtokS[cx["st"]], xTS[cx["st"]]
            for c in range(NFF):
                slot = self.wload(wgu[fi], Sel(c, lambda a, c=c: a[c]), 2048, "F")
                wv = slot[:].rearrange("p (g k j) -> p g k j", g=2, k=KC)
                pg = pbank(cx)
                pu = pbank(cx)
                k.mm(pg[:, 0:tw], [(wv[:, 0, kc, :], xT[:, kc, 0:tw]) for kc in range(KC)])
                k.mm(pu[:, 0:tw], [(wv[:, 1, kc, :], xT[:, kc, 0:tw]) for kc in range(KC)])
                sgb = sg[c % 2]
                k.actf(sgb[:, 0:tw], pg[:, 0:tw], AF.Tanh, scale=0.5)
                k.stt(sgb[:, 0:tw], sgb[:, 0:tw], 1.0, pg[:, 0:tw], ALU.add, ALU.mult)
                k.stt(hT[:, c, 0:tw], sgb[:, 0:tw], 0.25, pu[:, 0:tw], ALU.mult, ALU.mult)
                yield
            kgs = [(0, 4), (4, 8), (8, 12), (12, 16), (16, 20), (20, 22)]
            for half in range(2):
                banks = [pbank(cx) for _ in range(ns)]
                for gi, (k0, k1) in enumerate(kgs):
                    n = (k1 - k0) * 512
                    slot = self.wload(wd[fi], Sel((half, k0), lambda a, half=half, k0=k0, k1=k1: a[half][:, k0 * 512:k1 * 512]), n, "F")
                    wv = slot[:, 0:n].rearrange("p (k j) -> p k j", j=512)
                    for s in range(ns):
                        k.mm(banks[s][:], [(hT[:, kc, s * 128:(s + 1) * 128], wv[:, kc - k0, :]) for kc in range(k0, k1)],
                             start=(gi == 0), stop=(gi == len(kgs) - 1))
                for s in range(ns):
                    hs = slice(half * 512, (half + 1) * 512)
                    k.stt(xtok[:, s, hs], xtok[:, s, hs], ALPHA, banks[s][:], ALU.mult, ALU.add)
                yield

        def layer_norm(cx, s, lnb):
            x = xtokS[cx["st"]][:, s, :]
            st6, mv, rstd, nmr = lnst[cx["pool"]][s % 2]
            for hf in range(2):
                k.op("dve", lambda e, hf=hf: e.bn_stats(out=st6[:, hf, :], in_=x[:, hf * 512:(hf + 1) * 512]),
                     outs=[st6[:, hf, :]], ins=[x[:, hf * 512:(hf + 1) * 512]])
            k.op("dve", lambda e: e.bn_aggr(out=mv[:], in_=st6[:].rearrange("p a b -> p (a b)")), outs=[mv[:]], ins=[st6[:]])
            k.ts(rstd[:], mv[:, 1:2], EPS, None, ALU.add)
            k.actf(rstd[:], rstd[:], AF.Sqrt)
            k.op("dve", lambda e: e.reciprocal(out=rstd[:], in_=rstd[:]), outs=[rstd[:]], ins=[rstd[:]])
            k.stt(nmr[:], mv[:, 0:1], -1.0, rstd[:], ALU.mult, ALU.mult)
            k.actf(x, x, AF.Identity, bias=nmr[:, 0:1], scale=rstd[:, 0:1])
            k.tt(x, x, lnb[0][:], ALU.mult)
            k.tt(x, x, lnb[1][:], ALU.add)

        def load_ln(lnb, li, tag):
            q_ = "pool" if tag == "M" else "sp"
            k.dma(lnb[0][:], lnp[2 * li], key=("ln", tag, 0), eng=q_)
            k.dma(lnb[1][:], lnp[2 * li + 1], key=("ln", tag, 1), eng=q_)

        class Chunks:
            def __init__(cs, src):
                cs.src = src
                cs.i = 0
                cs.slot = None

            def next(cs):
                if cs.i % 2 == 0:
                    cs.slot = self.wload(cs.src, Sel(cs.i // 2, lambda a, u=cs.i // 2: a[u]), 2048)
                v = cs.slot[:].rearrange("p (c k j) -> p c k j", c=2, k=KC)[:, cs.i % 2]
                cs.i += 1
                return v

        def proj_feat(cx, wv):
            tw = cx["tw"]
            xT = xTS[cx["st"]]
            pb = pbank(cx)
            k.mm(pb[:, 0:tw], [(wv[:, kc, :], xT[:, kc, 0:tw]) for kc in range(KC)])
            return pb

        def bc16(ap16, lo, n, w):
            return ap16[:, lo:lo + n].unsqueeze(2).broadcast_to([128, n, w])

        def ssd_sub(cx, s):
            smp = cx["smp"]
            xT = xTS[cx["st"]]
            tok = slice(s * 128, (s + 1) * 128)
            m_tri_b = triS_b if smp else tri_b
            m_ones_b = onesS_b if smp else ones_b
            m_tri_f = triS_f if smp else tri_f
            m_m1_b = m1S_b if smp else m1_b
            pd = PA[2]
            k.mm(pd[:, 0:16], [(xT[:, kc, tok], wdt[:, kc, :]) for kc in range(KC)])
            k.tt(dtr[:], pd[:, 0:16], dtb, ALU.add)
            k.actf(dtr[:], dtr[:], AF.Exp)
            k.actf(dtr[:], dtr[:], AF.Ln, bias=1.0)
            k.tt(adt[:], dtr[:], a_bc[:], ALU.mult)
            k.copy(adt_hi[:], adt[:])
            k.tt(adt_r[:], adt[:], adt_hi[:], ALU.subtract)
            k.copy(adt_lo[:], adt_r[:])
            pc = PA[1]
            k.mm(pc[:, 0:16], [(m_tri_b[:], adt_hi[:]), (m_tri_b[:], adt_lo[:])])
            k.mm(pc[:, 16:32], [(m_ones_b[:], adt_hi[:]), (m_ones_b[:], adt_lo[:])])
            k.actf(ea[:], pc[:, 0:16], AF.Exp)
            k.actf(decbc[:], pc[:, 16:32], AF.Exp)
            k.copy(acs[:], pc[:, 0:16], eng="act")
            k.tt(te[:], pc[:, 16:32], acs[:], ALU.subtract)
            k.actf(te[:], te[:], AF.Exp)
            k.tt(dtte[:], dtr[:], te[:], ALU.mult)
            yield
            pcb = PA[2]
            for g in range(4):
                k.mm(pcb[:, g * 128:(g + 1) * 128], [(xbcT[:, 8 + g, tok], xbcT[:, 12 + g, tok])])
            k.tt(CBm[:], pcb[:].rearrange("p (g t) -> p g t", g=4), m_tri_f[:].unsqueeze(1).broadcast_to([128, 4, 128]), ALU.mult)
            for hf in range(2):
                k.tt(rseg_hi.rearrange("p (h t) -> p h t", h=8), bc16(adt_hi, hf * 8, 8, 128),
                     m_tri_f[:].unsqueeze(1).broadcast_to([128, 8, 128]), ALU.mult)
                k.tt(rseg_lo.rearrange("p (h t) -> p h t", h=8), bc16(adt_lo, hf * 8, 8, 128),
                     m_tri_f[:].unsqueeze(1).broadcast_to([128, 8, 128]), ALU.mult)
                for q in range(2):
                    pq = PA[q + 3 * hf]
                    k.mm(pq[:], [(m_m1_b[:], rseg_hi[:, q * 512:(q + 1) * 512]), (m_m1_b[:], rseg_lo[:, q * 512:(q + 1) * 512])])
                    k.actf(eseg[:, q * 512:(q + 1) * 512], pq[:], AF.Exp)
                k.tt(WT[:, hf * 8:(hf + 1) * 8, :].rearrange("p (g r) t -> p g r t", g=2),
                     eseg.rearrange("p (g r t) -> p g r t", g=2, r=4),
                     CBm[:, hf * 2:hf * 2 + 2, :].unsqueeze(2).broadcast_to([128, 2, 4, 128]), ALU.mult)
                yield
            ptx = ptbank()
            for j in range(8):
                k.tr(ptx[:, j * 128:(j + 1) * 128], xbcT[:, j, tok], ident_b[:])
            ptb = ptbank()
            for g in range(4):
                k.tr(ptb[:, g * 128:(g + 1) * 128], xbcT[:, 8 + g, tok], ident_b[:])
            xv = ptx[:].rearrange("p (h j) -> p h j", h=16)
            k.tt(xs_.rearrange("p (h j) -> p h j", h=16), xv, bc16(dtr, 0, 16, 64), ALU.mult)
            k.tt(xD.rearrange("p (h j) -> p h j", h=16), xv, bc16(dsk, 0, 16, 64), ALU.mult)
            k.tt(xe.rearrange("p (h j) -> p h j", h=16), xv, bc16(dtte, 0, 16, 64), ALU.mult)
            k.copy(Btok[:].rearrange("p g n -> p (g n)"), ptb[:, 0:512], eng="act")
            yield
            for hf in range(2):
                py = PA[hf]
                k.mm(py[:], [(ident_b[:], xD[:, hf * 512:(hf + 1) * 512])], start=True, stop=False, skip=True)
                for hh in range(8):
                    h = hf * 8 + hh
                    k.mm(py[:, hh * 64:(hh + 1) * 64], [(WT[:, h, :], xs_[:, h * 64:(h + 1) * 64])], start=False, stop=(hh == 7),
                         skip=True)
                k.copy(ysum[:, hf * 512:(hf + 1) * 512], py[:], eng="act")
            yield
            pyi = [PA[0], PA[1]] if smp else [PA[3], PA[4]]
            if not smp:
                for g in range(4):
                    k.mm(pyi[g // 2][:, (g % 2) * 256:(g % 2 + 1) * 256], [(xbcT[:, 12 + g, tok], hstT_b[:, g * 256:(g + 1) * 256])])
            else:
                k.copy(adt_hi[:], decbc[:])
                k.tt(adt_r[:], decbc[:], adt_hi[:], ALU.subtract)
                k.copy(adt_lo[:], adt_r[:])
                k.copy(rseg_hi.rearrange("p (h j) -> p h j", h=16), bc16(adt_hi, 0, 16, 64))
                k.copy(rseg_lo.rearrange("p (h j) -> p h j", h=16), bc16(adt_lo, 0, 16, 64))
                pdc = PA[2]
                for j in range(8):
                    k.mm(pdc[:, j * 16:(j + 1) * 16], [(rseg_hi[:, j * 128:(j + 1) * 128], sel16[:]),
                                                        (rseg_lo[:, j * 128:(j + 1) * 128], sel16[:])])
                k.copy(deccol[:].rearrange("p j q -> p (j q)"), pdc[:, 0:128], eng="act")
                for hb in range(2):
                    k.mm(pyi[hb][:], [(zero_b[:], xs_[:, 0:512])], start=True, stop=False, skip=True)
                hbufs = [hstT, xtokS[cx["st"]][:, 1, :]]
                for q in range(16):
                    hq = hbufs[q % 2]
                    k.dma(hq.rearrange("p (j n) -> p j n", j=8) if q % 2 else hstT[:].rearrange("p (j n) -> p j n", j=8),
                          sssm_d[q].rearrange("(j p) n -> p j n", p=128), key=("ssin", q % 2), eng="act")
                    for half in range(2):
                        pb = PA[2 + half]
                        for jj in range(4):
                            j = half * 4 + jj
                            k.tr(pb[:, jj * 128:(jj + 1) * 128], hq[:, j * 128:(j + 1) * 128], ident_f[:])
                        k.copy(hstT_b[:, half * 512:(half + 1) * 512], pb[:], eng="act")
                    k.tt(cmT[:], xbcT[:, 12:16, tok], seqrow[:, q, :].unsqueeze(1).broadcast_to([128, 4, 128]), ALU.mult)
                    for g in range(4):
                        k.mm(pyi[g // 2][:, (g % 2) * 256:(g % 2 + 1) * 256], [(cmT[:, g, :], hstT_b[:, g * 256:(g + 1) * 256])],
                             start=False, stop=(q == 15 and g % 2 == 1), skip=True)
                    k.actf(xem, xe, AF.Copy, scale=seqcol[:, q:q + 1])
                    for half in range(2):
                        pst = PA[4 + half]
                        for jj in range(4):
                            j = half * 4 + jj
                            k.mm(pst[:, jj * 128:(jj + 1) * 128], [(xem[:, j * 128:(j + 1) * 128], Btok[:, j // 2, :])])
                        for jj in range(4):
                            j = half * 4 + jj
                            k.stt(osq[:, j * 128:(j + 1) * 128], hq[:, j * 128:(j + 1) * 128], deccol[:, j, q:q + 1],
                                  pst[:, jj * 128:(jj + 1) * 128], ALU.mult, ALU.add)
                    k.dma(ssm_s[q].rearrange("(j p) n -> p j n", p=128), osq.rearrange("p (j n) -> p j n", j=8),
                          key="ssout", eng="act", sb_out=False)
                    yield
            for hf in range(2):
                hs = slice(hf * 512, (hf + 1) * 512)
                k.tt(yi[:, hs].rearrange("p (h j) -> p h j", h=8), pyi[hf][:].rearrange("p (h j) -> p h j", h=8),
                     bc16(ea, hf * 8, 8, 64), ALU.mult)
                k.tt(ysum[:, hs], ysum[:, hs], yi[:, hs], ALU.add)
            yield
            if not smp:
                for hf in range(2):
                    pst = PA[2 + 3 * hf]
                    for gg in range(2):
                        g = hf * 2 + gg
                        k.mm(pst[:, gg * 256:(gg + 1) * 256], [(Btok[:, g, :], xe[:, g * 256:(g + 1) * 256])])
                    hs = slice(hf * 512, (hf + 1) * 512)
                    k.tt(hstT[:, hs].rearrange("p (h j) -> p h j", h=8), hstT[:, hs].rearrange("p (h j) -> p h j", h=8),
                         bc16(decbc, hf * 8, 8, 64), ALU.mult)
                    k.tt(hstT[:, hs], hstT[:, hs], pst[:], ALU.add)
                    k.copy(hstT_b[:, hs], hstT[:, hs], eng="act")
                yield
            k.stt(ysum, ysum, 0.5, zs[:, s, :], ALU.mult, ALU.mult)
            k.tt(yi, ysum, ysum, ALU.mult)
            k.op("dve", lambda e: e.tensor_reduce(out=ssq[:], in_=yi.rearrange("p (g j) -> p g j", g=4),
                                                   axis=mybir.AxisListType.X, op=ALU.add),
                 outs=[ssq[:]], ins=[yi])
            k.ts(ssq[:], ssq[:], 1.0 / 256.0, EPS, ALU.mult, ALU.add)
            k.actf(ssq[:], ssq[:], AF.Sqrt)
            k.op("dve", lambda e: e.reciprocal(out=ssq[:], in_=ssq[:]), outs=[ssq[:]], ins=[ssq[:]])
            k.tt(gn[:].rearrange("p (g j) -> p g j", g=4), ysum.rearrange("p (g j) -> p g j", g=4),
                 ssq[:].unsqueeze(2).broadcast_to([128, 4, 256]), ALU.mult)
            ptg = ptbank()
            for j in range(8):
                k.tr(ptg[:, j * 128:(j + 1) * 128], gn[:, j * 128:(j + 1) * 128], ident_b[:])
            k.tt(gatedT[:, :, tok], ptg[:].rearrange("p (j t) -> p j t", j=8), mnw.unsqueeze(2).broadcast_to([128, 8, 128]), ALU.mult)
            yield

        def hgrn_prep(cx, h, pf, pq, pog):
            tw, cl = cx["tw"], cx["cl"]
            nch = tw // cl
            rm = resetmS if cx["smp"] else resetm
            a = dict((n, v[:, 0:tw]) for n, v in A.items())
            k.actf(a["sig"], pf[:, 0:tw], AF.Tanh, scale=0.5)
            k.ts(a["f"], a["sig"], omlh[:, h:h + 1], lbh[:, h:h + 1], ALU.mult, ALU.add)
            k.op("dve", lambda e: e.tensor_tensor_scan(out=a["b"], data0=rm[:, 0:tw], data1=a["f"],
                                                        initial=1.0, op0=ALU.max, op1=ALU.mult),
                 outs=[a["b"]], ins=[rm[:, 0:tw], a["f"]])
            k.ts(a["f"], a["f"], -1.0, 1.0, ALU.mult, ALU.add)
            k.op("dve", lambda e: e.reciprocal(out=a["e2"], in_=a["b"]), outs=[a["e2"]], ins=[a["b"]])
            bv = a["b"].rearrange("p (c j) -> p c j", j=cl)
            k.tt(a["e3"].rearrange("p (c j) -> p c j", j=cl), bv[:, :, cl - 1:cl].broadcast_to([128, nch, cl]),
                 a["e2"].rearrange("p (c j) -> p c j", j=cl), ALU.mult)
            k.copy(decT[:, h, 0:nch], bv[:, :, cl - 1], eng="act")
            k.tt(qdT[:, h, 0:tw], pq[:, 0:tw], a["b"], ALU.mult)
            k.tt(kdT[:, h, 0:tw], a["f"], a["e2"], ALU.mult)
            k.tt(keT[:, h, 0:tw], a["f"], a["e3"], ALU.mult)
            k.actf(a["e1"], pog[:, 0:tw], AF.Tanh, scale=0.5)
            k.stt(sog[:, h, 0:tw], a["e1"], 1.0, pog[:, 0:tw], ALU.add, ALU.mult)

        def hgrn_sub(cx, s):
            smp = cx["smp"]
            tok = slice(s * 128, (s + 1) * 128)
            msk = triS_f if smp else bmask
            psc = [PA[3], PA[4]]
            for h in range(8):
                k.mm(psc[h // 4][:, (h % 4) * 128:(h % 4 + 1) * 128], [(kdT[:, h, tok], qdT[:, h, tok])])
            for hb in range(2):
                k.tt(scT[:, hb * 4:hb * 4 + 4, :], psc[hb][:].rearrange("p (h t) -> p h t", h=4),
                     msk[:].unsqueeze(1).broadcast_to([128, 4, 128]), ALU.mult)
            ptk = ptbank()
            for h in range(8):
                k.tr(ptk[:, h * 128:(h + 1) * 128], keT[:, h, tok], ident_b[:])
            k.copy(ketok.rearrange("p h k -> p (h k)"), ptk[:], eng="act")
            if not smp:
                for c in range(4):
                    k.actf(vm[:, c, :], vtok[:, s, :], AF.Copy, scale=cmk[:, c:c + 1])
            yield
            po = [PA[0], PA[1]]
            for hb in range(2):
                k.mm(po[hb][:], [(zero_b[:], gn[:, 0:512])], start=True, stop=False, skip=True)
            for h in range(8):
                k.mm(po[h // 4][:, (h % 4) * 128:(h % 4 + 1) * 128], [(vtok[:, s, h * 128:(h + 1) * 128], scT[:, h, :])],
                     start=False, stop=False, skip=True)
            pSs = [PA[2], PA[3]]
            if not smp:
                for c in range(4):
                    ci = s * 4 + c
                    for h in range(8):
                        col = (h % 4) * 128 + 32 * c
                        k.mm(po[h // 4][:, col:col + 32], [(Sst_b[:, h, :], qdT[:, h, s * 128 + 32 * c:s * 128 + 32 * c + 32])],
                             start=False, stop=(c == 3 and h % 4 == 3), skip=True)
                        pS = pSs[h // 4]
                        k.mm(pS[:, (h % 4) * 128:(h % 4 + 1) * 128], [(ketok[:, h, :], vm[:, c, h * 128:(h + 1) * 128])])
                        if h % 4 == 3:
                            for h2 in range(h - 3, h + 1):
                                k.stt(Sst[:, h2, :], Sst[:, h2, :], decT[:, h2, ci:ci + 1], pS[:, (h2 % 4) * 128:(h2 % 4 + 1) * 128],
                                      ALU.mult, ALU.add)
                            k.copy(Sst_b[:, h - 3:h + 1, :], Sst[:, h - 3:h + 1, :], eng="act")
                    yield
            else:
                for q in range(16):
                    k.dma(Sst[:], shg_d[q].rearrange("(h k) v -> k h v", k=128), key="hgin", eng="act")
                    k.copy(Sst_b[:].rearrange("p h v -> p (h v)"), Sst[:].rearrange("p h v -> p (h v)"), eng="act")
                    k.actf(vm[:, 0, :], vtok[:, 0, :], AF.Copy, scale=seqcol[:, q:q + 1])
                    for h in range(8):
                        col = (h % 4) * 128 + 4 * q
                        k.mm(po[h // 4][:, col:col + 4], [(Sst_b[:, h, :], qdT[:, h, 4 * q:4 * q + 4])], start=False,
                             stop=(q == 15 and h % 4 == 3), skip=True)
                        pS = pSs[h // 4]
                        k.mm(pS[:, (h % 4) * 128:(h % 4 + 1) * 128], [(ketok[:, h, :], vm[:, 0, h * 128:(h + 1) * 128])])
                        if h % 4 == 3:
                            for h2 in range(h - 3, h + 1):
                                k.stt(Sst[:, h2, :], Sst[:, h2, :], decT[:, h2, q:q + 1], pS[:, (h2 % 4) * 128:(h2 % 4 + 1) * 128],
                                      ALU.mult, ALU.add)
                    k.dma(hg_s[q].rearrange("(h k) v -> k h v", k=128), Sst[:], key="hgout", eng="act", sb_out=False)
                    yield
            for hb in range(2):
                hs = slice(hb * 512, (hb + 1) * 512)
                k.actf(osqb, po[hb][:], AF.Square)
                pss = PA[2 + hb]
                k.mm(pss[:], [(ones_b[:], osqb)])
                k.ts(rs_[:, hs], pss[:], 1.0 / 128.0, EPS, ALU.mult, ALU.add)
                k.actf(rs_[:, hs], rs_[:, hs], AF.Sqrt)
                k.op("dve", lambda e, hs=hs: e.reciprocal(out=rs_[:, hs], in_=rs_[:, hs]), outs=[rs_[:, hs]], ins=[rs_[:, hs]])
                k.tt(t1[:, hs], po[hb][:], rs_[:, hs], ALU.mult)
            for h in range(8):
                k.stt(ofinT[:, h, tok], t1[:, h * 128:(h + 1) * 128], hnwh[:, h:h + 1], sog[:, h, tok], ALU.mult, ALU.mult)
            yield

        def conv_A(cx, c, pp):
            tw, smp = cx["tw"], cx["smp"]
            pr = pre[c % 2]
            if not smp:
                k.copy(pr[:, 0:3], carry[:, c, :])
                k.copy(pr[:, 3:3 + tw], pp[:, 0:tw], eng="act")
                k.copy(carry[:, c, :], pr[:, tw:tw + 3])
                return
            pr3 = pr[:, 0:112].rearrange("p (q j) -> p q j", j=7)
            src = ysum if c < 8 else yi
            ph = pbank(cx)
            k.tr(ph[:, 0:128], src[:, (c % 8) * 128:(c % 8 + 1) * 128], ident_f[:])
            k.copy(pr3[:, :, 0:3], ph[:, 0:48].rearrange("p (q j) -> p q j", j=3))
            k.copy(pr3[:, :, 3:7], pp[:, 0:64].rearrange("p (q j) -> p q j", j=4), eng="act")
            k.copy(tmpa[:, 0:128], pp[:, 0:128], eng="act")
            pt2 = pbank(cx)
            k.tr(pt2[:, 0:128], tmpa[:, 0:128], ident_f[:])
            stg = hstT if c < 8 else Sst[:].rearrange("p h v -> p (h v)")
            k.copy(stg[:, (c % 8) * 128:(c % 8 + 1) * 128], pt2[:, 0:128])

        def conv_B(cx, c):
            tw, smp = cx["tw"], cx["smp"]
            pr = pre[c % 2]
            ca = cacc[c % 2]
            if not smp:
                k.ts(ca[:, 0:tw], pr[:, 0:tw], cw[:, c, 0:1], cbh[:, c:c + 1], ALU.mult, ALU.add)
                for kk in range(1, 4):
                    k.stt(ca[:, 0:tw], pr[:, kk:kk + tw], cw[:, c, kk:kk + 1], ca[:, 0:tw], ALU.mult, ALU.add)
                k.actf(pr[:, 0:tw], ca[:, 0:tw], AF.Tanh)
                k.stt(xbcT[:, c, 0:tw], pr[:, 0:tw], 1.0, ca[:, 0:tw], ALU.add, ALU.mult)
                return
            pr3 = pr[:, 0:112].rearrange("p (q j) -> p q j", j=7)
            ca3 = ca[:, 0:64].rearrange("p (q j) -> p q j", j=4)
            k.ts(ca3, pr3[:, :, 0:4], cw[:, c, 0:1], cbh[:, c:c + 1], ALU.mult, ALU.add)
            for kk in range(1, 4):
                k.stt(ca3, pr3[:, :, kk:kk + 4], cw[:, c, kk:kk + 1], ca3, ALU.mult, ALU.add)
            k.actf(pr[:, 112:176], ca[:, 0:64], AF.Tanh)
            k.stt(xbcT[:, c, 0:64], pr[:, 112:176], 1.0, ca[:, 0:64], ALU.add, ALU.mult)

        def mixer(cx):
            tw, ns, smp = cx["tw"], cx["ns"], cx["smp"]
            xtok, xT = xtokS[cx["st"]], xTS[cx["st"]]
            for g in range(4):
                banks = [pbank(cx) for _ in range(ns)]
                for kh in range(2):
                    slot = self.wload(wtok, Sel((g, kh), lambda a, g=g, kh=kh: a[g][:, kh * 2048:(kh + 1) * 2048]), 2048)
                    wv = slot[:].rearrange("p (k j) -> p k j", j=512)
                    for s in range(ns):
                        k.mm(banks[s][:], [(xT[:, kh * 4 + kc, s * 128:(s + 1) * 128], wv[:, kc, :]) for kc in range(4)],
                             start=(kh == 0), stop=(kh == 1))
                hs = slice((g % 2) * 512, (g % 2 + 1) * 512)
                for s in range(ns):
                    if g < 2:
                        k.actf(yi[:, 0:512], banks[s][:], AF.Tanh, scale=0.5)
                        k.stt(zs[:, s, hs], yi[:, 0:512], 1.0, banks[s][:], ALU.add, ALU.mult)
                    else:
                        k.copy(vtok[:, s, hs], banks[s][:])
                yield
            cs = Chunks(winf)
            if smp:
                k.dma(ysum, sconv_d[:, 0:1024], key="scv0", eng="act")
                k.dma(yi, sconv_d[:, 1024:2048], key="scv1", eng="act")
                k.memset(xbcT[:, :, 64:128], 0.0)
            for c in range(16):
                conv_A(cx, c, proj_feat(cx, cs.next()))
                if c > 0:
                    conv_B(cx, c - 1)
                yield
            conv_B(cx, 15)
            if smp:
                cv = conv_s.rearrange("(q j) c -> q j c", j=3)
                for j in range(1, 4):
                    for half in range(2):
                        stg = hstT if half == 0 else Sst[:].rearrange("p h v -> p (h v)")
                        k.dma(cv[:, j - 1, half * 1024:(half + 1) * 1024], stg[j:64:4, :], key=("cvo", half, j), eng="act",
                              sb_out=False)
            for s in range(ns):
                yield from ssd_sub(cx, s)
            for h in range(8):
                pf = proj_feat(cx, cs.next())
                pq = proj_feat(cx, cs.next())
                pog = proj_feat(cx, cs.next())
                hgrn_prep(cx, h, pf, pq, pog)
                yield
            for s in range(ns):
                yield from hgrn_sub(cx, s)
            for j in range(8):
                pg = proj_feat(cx, cs.next())
                k.actf(sgm[:, j, 0:tw], pg[:, 0:tw], AF.Tanh, scale=0.5)
                k.actf(sgm[:, j, 0:tw], sgm[:, j, 0:tw], AF.Identity, bias=0.5, scale=0.5)
                yield
            for j in range(8):
                pg = proj_feat(cx, cs.next())
                k.actf(sgh[:, j, 0:tw], pg[:, 0:tw], AF.Tanh, scale=0.5)
                k.actf(sgh[:, j, 0:tw], sgh[:, j, 0:tw], AF.Identity, bias=0.5, scale=0.5)
                yield
            cm_ = Chunks(wmo)
            ch_ = Chunks(who)
            for j in range(8):
                wm = cm_.next()
                wh = ch_.next()
                pm = pbank(cx)
                ph = pbank(cx)
                k.mm(pm[:, 0:tw], [(wm[:, kc, :], gatedT[:, kc, 0:tw]) for kc in range(KC)])
                k.mm(ph[:, 0:tw], [(wh[:, kc, :], ofinT[:, kc, 0:tw]) for kc in range(KC)])
                k.tt(tmpa[:, 0:tw], pm[:, 0:tw], sgm[:, j, 0:tw], ALU.mult)
                k.tt(tmpb[:, 0:tw], ph[:, 0:tw], sgh[:, j, 0:tw], ALU.mult)
                k.tt(mixT[:, j, 0:tw], tmpa[:, 0:tw], tmpb[:, 0:tw], ALU.add)
                yield
            for half in range(2):
                banks = [pbank(cx) for _ in range(ns)]
                for kh in range(2):
                    slot = self.wload(wo, Sel((half, kh), lambda a, half=half, kh=kh: a[half][:, kh * 2048:(kh + 1) * 2048]), 2048)
                    wv = slot[:].rearrange("p (k j) -> p k j", j=512)
                    for s in range(ns):
                        k.mm(banks[s][:], [(mixT[:, kh * 4 + kc, s * 128:(s + 1) * 128], wv[:, kc, :]) for kc in range(4)],
                             start=(kh == 0), stop=(kh == 1))
                hs = slice(half * 512, (half + 1) * 512)
                for s in range(ns):
                    k.stt(xtok[:, s, hs], xtok[:, s, hs], ALPHA, banks[s][:], ALU.mult, ALU.add)
                yield

        def store_y(cx):
            xtok = xtokS[cx["st"]]
            if cx["smp"]:
                k.dma(y_s[:, :], xtok[0:64, 0, :], key=("y", cx["st"], 0), sb_out=False)
                return
            for s in range(NS):
                r0 = cx["ti"] * T + s * 128
                k.dma(y_p[r0:r0 + 128, :], xtok[:, s, :], key=("y", cx["st"], s), sb_out=False)

        def final_prompt_states(cx):
            for q in range(4):
                pb = pbank(cx)
                for j in range(4):
                    c = q * 4 + j
                    k.tr(pb[0:3, j * 128:(j + 1) * 128], carry[:, c, :], ident_f[:])
                k.copy(osq[0:3, 0:512], pb[0:3, :], eng="act")
                k.dma(conv_p[:, q * 512:(q + 1) * 512], osq[0:3, 0:512], key="oc", eng="pool", sb_out=False)
            for q in range(2):
                pb = pbank(cx)
                for j in range(4):
                    c = q * 4 + j
                    k.tr(pb[:, j * 128:(j + 1) * 128], hstT[:, c * 128:(c + 1) * 128], ident_f[:])
                k.copy(outst[:, q * 512:(q + 1) * 512], pb[:], eng="act")
            k.dma(ssm_p.rearrange("(c p) n -> p c n", p=128), outst.rearrange("p (c n) -> p c n", n=128), key="os", eng="pool", sb_out=False)
            k.dma(hg_p.rearrange("(h k) v -> k h v", k=128), Sst[:], key="oh", eng="pool", sb_out=False)

        tiles = [dict(ti=t, ns=NS, tw=T, smp=False, cl=32, st=t % 2) for t in range(NTILE)]
        tiles.append(dict(ti=NTILE, ns=1, tw=128, smp=True, cl=4, st=NTILE % 2))
        if mode == "sample_only":
            tiles = [dict(ti=0, ns=1, tw=128, smp=True, cl=4, st=0)]
        NTT = len(tiles)

        def F1(t):
            cx = dict(tiles[t], pool="F")
            load_x(cx)
            yield from make_xT(cx)
            load_ln(lnF, 0, "F")
            yield from ffn(cx, 0)
            for s in range(cx["ns"]):
                layer_norm(cx, s, lnF)
                yield
            yield from make_xT(cx)

        def Mx(t):
            cx = dict(tiles[t], pool="M")
            load_ln(lnM, 1, "M")
            yield from mixer(cx)
            for s in range(cx["ns"]):
                layer_norm(cx, s, lnM)
                yield
            yield from make_xT(cx)
            if not cx["smp"] and t + 1 < NTT and tiles[t + 1]["smp"]:
                final_prompt_states(cx)
                yield

        def F2(t):
            cx = dict(tiles[t], pool="F")
            load_ln(lnF, 2, "F")
            yield from ffn(cx, 1)
            for s in range(cx["ns"]):
                layer_norm(cx, s, lnF)
                yield
            store_y(cx)
            yield

        def chain(*gens):
            for g in gens:
                yield from g

        def run_pair(ga, gb):
            la = list_len.get(ga[0], 1)
            lb_ = list_len.get(gb[0], 1)
            ia = ib = 0
            a, b = ga[1], gb[1]
            da = db = False
            while not (da and db):
                if not da and (db or ia * lb_ <= ib * la):
                    try:
                        next(a)
                        ia += 1
                    except StopIteration:
                        da = True
                elif not db:
                    try:
                        next(b)
                        ib += 1
                    except StopIteration:
                        db = True

        list_len = {"M": 110, "Ms": 150, "F": 150, "F1": 75, "F2": 75}

        for _ in F1(0):
            pass
        for kslot in range(NTT + 1):
            m_gen = Mx(kslot) if kslot < NTT else None
            f_parts = []
            if kslot >= 1:
                f_parts.append(F2(kslot - 1))
            if kslot + 1 < NTT:
                f_parts.append(F1(kslot + 1))
            f_gen = chain(*f_parts) if f_parts else None
            if m_gen is not None and f_gen is not None:
                mk = "Ms" if tiles[kslot]["smp"] else "M"
                fk = "F" if len(f_parts) == 2 else "F1"
                run_pair((mk, m_gen), (fk, f_gen))
            elif m_gen is not None:
                for _ in m_gen:
                    pass
            elif f_gen is not None:
                for _ in f_gen:
                    pass

        k.emit()
        return nc


def _wgu_layout(wg, wu):
    g = wg.reshape(KC, 128, NFF, 128)
    u = wu.reshape(KC, 128, NFF, 128)
    a = np.stack([g, u], axis=0)
    a = a.transpose(3, 2, 0, 1, 4)
    return np.ascontiguousarray(a).reshape(NFF, 128, 2048)


def _wd_layout(wd):
    a = wd.reshape(NFF, 128, 2, 512).transpose(2, 1, 0, 3)
    return np.ascontiguousarray(a).reshape(2, 128, NFF * 512)


def _chunks_layout(w):
    n = w.shape[1] // 128
    return w.reshape(KC, 128, n, 128).transpose(2, 1, 0, 3)


def _pack4(ch):
    n = ch.shape[0]
    a = ch.reshape(n // 2, 2, 128, KC, 128).transpose(0, 2, 1, 3, 4)
    return np.ascontiguousarray(a).reshape(n // 2, 128, 2048)


def _half_layout(w):
    a = w.reshape(KC, 128, 2, 512).transpose(2, 1, 0, 3)
    return np.ascontiguousarray(a).reshape(2, 128, 4096)


_CACHE = {}


def _get_nc(mode):
    if mode not in _CACHE:
        _CACHE[mode] = Builder(mode).build()
    return _CACHE[mode]


def kernel(**inp):
    mode = inp.pop("_mode", "full")
    f = lambda a: np.ascontiguousarray(np.asarray(a, dtype=np.float32))
    x_prompt = f(inp["x_prompt"])
    w_in = f(inp["w_in"])[0]
    o_z, o_xbc, o_dt, o_q, o_f, o_i, o_og, o_gm, o_gh = 0, 1024, 3072, 3088, 4112, 5136, 6160, 7184, 8208
    fchunks = [_chunks_layout(w_in[:, o_xbc:o_xbc + 2048])]
    cf = _chunks_layout(w_in[:, o_f:o_f + 1024])
    cq = _chunks_layout(w_in[:, o_q:o_q + 1024])
    cog = _chunks_layout(w_in[:, o_og:o_og + 1024])
    for h in range(8):
        fchunks += [cf[h:h + 1], cq[h:h + 1], cog[h:h + 1]]
    fchunks += [_chunks_layout(w_in[:, o_gm:o_gm + 1024]), _chunks_layout(w_in[:, o_gh:o_gh + 1024])]
    winf = _pack4(np.concatenate(fchunks, axis=0))
    wtok = np.concatenate([_half_layout(w_in[:, o_z:o_z + 1024]), _half_layout(w_in[:, o_i:o_i + 1024])], axis=0)
    wdt = np.ascontiguousarray(w_in[:, o_dt:o_dt + 16].reshape(KC, 128, 16).transpose(1, 0, 2)).reshape(128, 128)
    smallp = np.zeros((128, 160), np.float32)
    smallp[:, 0:64] = f(inp["conv_w"])[0].reshape(4, 16, 128).transpose(2, 1, 0).reshape(128, 64)
    smallp[:, 64:80] = f(inp["conv_b"])[0].reshape(16, 128).T
    smallp[:, 80:96] = f(inp["dt_bias"])[0][None, :]
    smallp[:, 96:112] = f(inp["a_log"])[0][None, :]
    smallp[:, 112:128] = f(inp["d_skip"])[0][None, :]
    smallp[:, 128:136] = f(inp["m_norm_w"])[0].reshape(8, 128).T
    smallp[:, 136:144] = f(inp["h_norm_w"])[0].reshape(8, 128).T
    smallp[:, 144:160] = f(inp["hgrn_lb_param"]).reshape(2, 8, 128).transpose(2, 1, 0).reshape(128, 16)
    shared = {
        "wgu1": _wgu_layout(f(inp["ffn1_w_gate"])[0], f(inp["ffn1_w_up"])[0]),
        "wd1": _wd_layout(f(inp["ffn1_w_down"])[0]),
        "wgu2": _wgu_layout(f(inp["ffn2_w_gate"])[0], f(inp["ffn2_w_up"])[0]),
        "wd2": _wd_layout(f(inp["ffn2_w_down"])[0]),
        "lnp": np.ascontiguousarray(np.broadcast_to(
            np.stack([f(inp[n])[0] for n in ("ln1_g", "ln1_b", "ln2_g", "ln2_b", "ln3_g", "ln3_b")])[:, None, :],
            (6, 128, D))),
        "wtok": wtok, "wdt": wdt, "winf": winf,
        "wmo": _pack4(_chunks_layout(f(inp["w_m_out"])[0])),
        "who": _pack4(_chunks_layout(f(inp["w_h_out"])[0])),
        "wo": _half_layout(f(inp["w_o"])[0]),
        "smallp": smallp,
    }
    x_sample = f(inp["x_sample"])
    state_conv = f(inp["state_conv"])[0]
    state_ssm = f(inp["state_ssm"])[0]
    state_hgrn = f(inp["state_hgrn"])[0]
    in_maps = []
    for c in range(NCORES):
        m = dict(shared)
        m["xp"] = x_prompt[c]
        xs = np.zeros((128, D), np.float32)
        xs[0:64] = x_sample[16 * c:16 * c + 16].reshape(64, D)
        m["xs"] = xs
        sc = np.zeros((128, 2048), np.float32)
        sc[0:48] = state_conv[16 * c:16 * c + 16].reshape(48, 2048)
        m["sconv"] = sc
        m["sssm"] = np.ascontiguousarray(state_ssm[16 * c:16 * c + 16].reshape(16, 1024, 128))
        m["shg"] = np.ascontiguousarray(state_hgrn[16 * c:16 * c + 16].reshape(16, 1024, 128))
        in_maps.append(m)
    nc = _get_nc(mode)
    res = run_bass_kernel_spmd(nc, in_maps, core_ids=list(range(NCORES)))
    r = res.results
    y_p = np.stack([r[c]["y_p"] for c in range(NCORES)])
    conv_p = np.stack([r[c]["conv_p"] for c in range(NCORES)])[None]
    ssm_p = np.stack([r[c]["ssm_p"] for c in range(NCORES)]).reshape(1, NCORES, 16, 64, 128)
    hg_p = np.stack([r[c]["hg_p"] for c in range(NCORES)]).reshape(1, NCORES, 8, 128, 128)
    y_s = np.concatenate([r[c]["y_s"] for c in range(NCORES)]).reshape(128, 4, D)
    conv_s = np.concatenate([r[c]["conv_s"] for c in range(NCORES)]).reshape(1, 128, 3, 2048)
    ssm_s = np.concatenate([r[c]["ssm_s"] for c in range(NCORES)]).reshape(1, 128, 16, 64, 128)
    hg_s = np.concatenate([r[c]["hg_s"] for c in range(NCORES)]).reshape(1, 128, 8, 128, 128)
    return (y_p, y_s, conv_p, ssm_p, hg_p, conv_s, ssm_s, hg_s)
```

```python
import numpy as np
import concourse.bass as bass
import concourse.mybir as mybir
from concourse.bass_utils import run_bass_kernel_spmd

F32 = mybir.dt.float32
BF16 = mybir.dt.bfloat16
AF = mybir.ActivationFunctionType
ALU = mybir.AluOpType

NCORES = 8
D = 1024
DFF = 2816
NFF = 22
KC = 8
SEQ = 2048
T = 256
NS = T // 128
NTILE = SEQ // T
ALPHA = float(2.0 ** 0.25)
EPS = 1e-5
GR = 64
NSLOT = 8
SLOT = 2048


def _esz(dt):
    return 2 if dt == BF16 else 4


class Sched:
    ENGS = ("pe", "act", "dve", "pool", "sp")

    def __init__(self, nc):
        self.nc = nc
        self.ops = {e: [] for e in self.ENGS}
        self.cnt = {e: 0 for e in self.ENGS}
        self.last_w = {}
        self.readers = {}
        self.waited = {e: {} for e in self.ENGS}
        self.dma_cnt = {}
        self.gcache = {}

    def gran(self, ap):
        key = (ap.tensor.name, ap.offset, tuple(ap.ap), str(ap.dtype))
        g = self.gcache.get(key)
        if g is not None:
            return g
        if ap.tensor.name.startswith("pa") or ap.tensor.name.startswith("pt"):
            g = frozenset([(ap.tensor.name, "bank")])
            self.gcache[key] = g
            return g
        dims = list(ap.ap)
        pstride = dims[0][0]
        off = ap.offset % pstride if pstride else ap.offset
        free = [(s, c) for (s, c) in dims[1:] if c > 1 or len(dims) == 2]
        esz = _esz(ap.dtype)
        name = ap.tensor.name
        out = set()
        if not free:
            free = [(1, 1)]
        inner_s, inner_c = free[-1]
        outer = free[:-1]
        span = (inner_c - 1) * abs(inner_s) + 1

        def rec(i, base):
            if i == len(outer):
                lo = base * esz
                hi = (base + span) * esz
                for gidx in range(lo // GR, (hi - 1) // GR + 1):
                    out.add((name, gidx))
                return
            s, c = outer[i]
            for j in range(c):
                rec(i + 1, base + j * s)
                if s == 0:
                    break

        rec(0, off)
        g = frozenset(out)
        self.gcache[key] = g
        return g

    def op(self, eng, fn, outs=(), ins=(), dma=None, extra=()):
        deps = {}

        def add(tok):
            if tok is None:
                return
            k = (tok[0], tok[1])
            if tok[2] > deps.get(k, 0):
                deps[k] = tok[2]

        for tk_ in extra:
            add(tk_)
        og = set()
        for ap in outs:
            og |= self.gran(ap)
        ig = set()
        for ap in ins:
            ig |= self.gran(ap)
        for g in list(ig):
            if g[1] == "bank":
                ig.discard(g)
                og.add(g)
        for g in ig:
            add(self.last_w.get(g))
        for g in og:
            add(self.last_w.get(g))
            r = self.readers.get(g)
            if r:
                for k, v in r.items():
                    if v > deps.get(k, 0):
                        deps[k] = v
        if dma is not None:
            n = self.dma_cnt.get(dma, 0) + 1
            self.dma_cnt[dma] = n
            tok = ("dma", dma, n)
        else:
            n = self.cnt[eng] + 1
            self.cnt[eng] = n
            tok = ("eng", eng, n)
        waits = []
        wd = self.waited[eng]
        for k, v in deps.items():
            if k[0] == "eng" and k[1] == "pe" and eng == "pe":
                continue
            if k[0] == "eng" and k[1] == eng and eng in ("dve", "act") and dma is None and v < self.cnt[eng] - 1:
                continue
            if v > wd.get(k, 0):
                wd[k] = v
                waits.append((k, v))
        self.ops[eng].append((waits, fn, tok))
        for g in og:
            self.last_w[g] = tok
            self.readers[g] = {}
        tk = (tok[0], tok[1])
        for g in ig:
            if g in og:
                continue
            r = self.readers.setdefault(g, {})
            if tok[2] > r.get(tk, 0):
                r[tk] = tok[2]
        return tok

    def mm(self, out, pairs, start=True, stop=True, skip=False):
        def fn(e):
            n = len(pairs)
            ins = None
            for i, (l, r) in enumerate(pairs):
                if skip:
                    ins = e.matmul(out, lhsT=l, rhs=r, start=(start and i == 0), stop=(stop and i == n - 1),
                                   skip_group_check=True)
                else:
                    ins = e.matmul(out, lhsT=l, rhs=r, start=(start and i == 0), stop=(stop and i == n - 1))
            return ins
        self.op("pe", fn, outs=[out], ins=[a for p in pairs for a in p])

    def tr(self, out, in_, ident):
        self.op("pe", lambda e: e.transpose(out, in_, ident), outs=[out], ins=[in_, ident])

    def actf(self, out, in_, func, bias=None, scale=None, eng="act"):
        kw = {}
        ins = [in_]
        if bias is not None:
            kw["bias"] = bias
            if not isinstance(bias, float):
                ins.append(bias)
        if scale is not None:
            kw["scale"] = scale
            if not isinstance(scale, float):
                ins.append(scale)
        self.op("act", lambda e: e.activation(out=out, in_=in_, func=func, **kw), outs=[out], ins=ins)

    def tt(self, out, in0, in1, op, eng="dve"):
        self.op(eng, lambda e: e.tensor_tensor(out=out, in0=in0, in1=in1, op=op), outs=[out], ins=[in0, in1])

    def ts(self, out, in0, s1, s2, op0, op1=None, eng="dve"):
        ins = [in0]
        if not isinstance(s1, float):
            ins.append(s1)
        if s2 is not None and not isinstance(s2, float):
            ins.append(s2)
        if op1 is None:
            self.op(eng, lambda e: e.tensor_scalar(out=out, in0=in0, scalar1=s1, scalar2=None, op0=op0), outs=[out], ins=ins)
        else:
            self.op(eng, lambda e: e.tensor_scalar(out=out, in0=in0, scalar1=s1, scalar2=s2, op0=op0, op1=op1), outs=[out], ins=ins)

    def stt(self, out, in0, scalar, in1, op0, op1, eng="dve"):
        ins = [in0, in1]
        if not isinstance(scalar, float):
            ins.append(scalar)
        self.op(eng, lambda e: e.scalar_tensor_tensor(out=out, in0=in0, scalar=scalar, in1=in1, op0=op0, op1=op1), outs=[out], ins=ins)

    def copy(self, out, in_, eng="dve"):
        if eng == "act":
            self.op("act", lambda e: e.activation(out=out, in_=in_, func=AF.Copy), outs=[out], ins=[in_])
        else:
            self.op(eng, lambda e: e.tensor_copy(out=out, in_=in_), outs=[out], ins=[in_])

    def memset(self, ap, val, eng="dve"):
        self.op(eng, lambda e: e.memset(ap, val), outs=[ap])

    def dma(self, out, in_, key, eng="sp", sb_out=True):
        if sb_out:
            self.op(eng, lambda e: e.dma_start(out=out, in_=in_), outs=[out], dma=key)
        else:
            self.op(eng, lambda e: e.dma_start(out=out, in_=in_), ins=[in_], dma=key)

    def emit(self):
        nc = self.nc
        sems = {}
        for e in self.ENGS:
            sems[("eng", e)] = nc.alloc_semaphore("s_" + e)
        for i, k in enumerate(self.dma_cnt):
            sems[("dma", k)] = nc.alloc_semaphore("d%d" % i)
        final = [(("dma", k), v) for k, v in self.dma_cnt.items()]

        def run(ename, eobj):
            for waits, fn, tok in self.ops[ename]:
                for k, v in waits:
                    eobj.wait_ge(sems[k], v * (16 if k[0] == "dma" else 1))
                ins = fn(eobj)
                ins.then_inc(sems[(tok[0], tok[1])], 16 if tok[0] == "dma" else 1)
            if ename == "sp":
                for k, v in final:
                    eobj.wait_ge(sems[k], v * 16)
                for e2 in self.ENGS:
                    if e2 != "sp" and self.cnt[e2] > 0:
                        eobj.wait_ge(sems[("eng", e2)], self.cnt[e2])

        with nc.Block() as block:
            @block.tensor
            def _(e):
                run("pe", e)

            @block.scalar
            def _(e):
                run("act", e)

            @block.vector
            def _(e):
                run("dve", e)

            @block.gpsimd
            def _(e):
                run("pool", e)

            @block.sync
            def _(e):
                run("sp", e)


class Builder:
    def __init__(self, mode="full"):
        self.mode = mode
        self.nc = bass.Bass("TRN2", target_bir_lowering=False)
        self.k = Sched(self.nc)
        self.wi = 0
        self.rings = {"M": [0, 1, 2, 3], "F": [4, 5, 6, 7]}
        self.wcnt = {"M": 0, "F": 0}
        self.conv_tok = {}
        self.cv_hist = []
        self.ncv = 0

    def din(self, name, shape, dt=F32):
        return self.nc.dram_tensor(name, list(shape), dt, kind="ExternalInput").ap()

    def dout(self, name, shape, dt=F32):
        return self.nc.dram_tensor(name, list(shape), dt, kind="ExternalOutput").ap()

    def sb(self, name, shape, dt=F32):
        return self.nc.alloc_sbuf_tensor(name, list(shape), dt)

    def ps(self, name, shape, dt=F32):
        return self.nc.alloc_psum_tensor(name, list(shape), dt)

    def wload(self, w, sel, n, pool="M"):
        k = self.k
        src, scr, name = w
        ukey = (name, self.unit_id(sel))
        tok = self.conv_tok.get(ukey)
        if tok is None:
            i = self.ncv
            self.ncv += 1
            prev = self.cv_hist[i - 4] if i >= 4 else None
            o_, i_ = sel(scr), sel(src)
            tok = k.op("pool", lambda e: e.dma_start(out=o_, in_=i_), dma=("cv", i % 4), extra=[prev] if prev else [])
            self.cv_hist.append(tok)
            self.conv_tok[ukey] = tok
        ring = self.rings[pool]
        sl = ring[self.wcnt[pool] % len(ring)]
        self.wcnt[pool] += 1
        dst = self.wslots[sl][:, 0:n]
        in_ = sel(scr)
        k.op("sp" if pool == "F" else "pool", lambda e: e.dma_start(out=dst, in_=in_), outs=[dst], dma=("w", sl), extra=[tok])
        return self.wslots[sl]

    def unit_id(self, sel):
        return sel.uid

    def build(self):
        nc, k = self.nc, self.k
        mode = self.mode
        xp = self.din("xp", [SEQ, D])
        xs_d = self.din("xs", [128, D])
        sconv_d = self.din("sconv", [128, 2048])
        sssm_d = self.din("sssm", [16, 1024, 128])
        shg_d = self.din("shg", [16, 1024, 128])
        def wpair(name, shape):
            return (self.din(name, shape), self.nc.dram_tensor(name + "_b16", list(shape), BF16, kind="Internal").ap(), name)

        class Sel:
            def __init__(se, uid, f):
                se.uid, se.f = uid, f

            def __call__(se, a):
                return se.f(a)

        wgu = [wpair("wgu1", [22, 128, 2048]), wpair("wgu2", [22, 128, 2048])]
        wd = [wpair("wd1", [2, 128, NFF * 512]), wpair("wd2", [2, 128, NFF * 512])]
        lnp = self.din("lnp", [6, 128, D])
        wtok = wpair("wtok", [4, 128, 4096])
        wdt_d = self.din("wdt", [128, 128])
        winf = wpair("winf", [28, 128, 2048])
        wmo = wpair("wmo", [4, 128, 2048])
        who = wpair("who", [4, 128, 2048])
        wo = wpair("wo", [2, 128, 4096])
        smallp = self.din("smallp", [128, 160])
        y_p = self.dout("y_p", [SEQ, D])
        conv_p = self.dout("conv_p", [3, 2048])
        ssm_p = self.dout("ssm_p", [1024, 128])
        hg_p = self.dout("hg_p", [1024, 128])
        y_s = self.dout("y_s", [64, D])
        conv_s = self.dout("conv_s", [48, 2048])
        ssm_s = self.dout("ssm_s", [16, 1024, 128])
        hg_s = self.dout("hg_s", [16, 1024, 128])

        self.wslots = [self.sb("wslot%d" % i, [128, SLOT], BF16) for i in range(NSLOT)]
        xtokS = [self.sb("xtok%d" % i, [128, NS, D]) for i in range(2)]
        xTS = [self.sb("xT%d" % i, [128, KC, T], BF16) for i in range(2)]
        hT = self.sb("hT", [128, NFF, T], BF16)
        sg = [self.sb("sg%d" % i, [128, T]) for i in range(2)]
        lnF = [self.sb("lnF%d" % i, [128, D]) for i in range(2)]
        lnM = [self.sb("lnM%d" % i, [128, D]) for i in range(2)]
        lnst = {}
        for pl in ("F", "M"):
            lnst[pl] = [(self.sb("st6%s%d" % (pl, i), [128, 2, 6]), self.sb("mv%s%d" % (pl, i), [128, 2]),
                         self.sb("rstd%s%d" % (pl, i), [128, 1]), self.sb("nmr%s%d" % (pl, i), [128, 1])) for i in range(2)]
        ident_f = self.sb("ident_f", [128, 128])
        ident_b = self.sb("ident_b", [128, 128], BF16)
        ones_f = self.sb("ones_f", [128, 128])
        tri_f = self.sb("tri_f", [128, 128])
        m1_f = self.sb("m1_f", [128, 128])
        bmask = self.sb("bmask", [128, 128])
        triS_f = self.sb("triS_f", [128, 128])
        cmk = self.sb("cmk", [128, 4])
        resetm = self.sb("resetm", [128, T])
        resetmS = self.sb("resetmS", [128, 128])
        tri_b = self.sb("tri_b", [128, 128], BF16)
        m1_b = self.sb("m1_b", [128, 128], BF16)
        ones_b = self.sb("ones_b", [128, 128], BF16)
        zero_b = self.sb("zero_b", [128, 128], BF16)
        triS_b = self.sb("triS_b", [128, 128], BF16)
        m1S_b = self.sb("m1S_b", [128, 128], BF16)
        onesS_b = self.sb("onesS_b", [128, 128], BF16)
        sel16 = self.sb("sel16", [128, 16], BF16)
        seqcol = self.sb("seqcol", [128, 16])
        seqrow = self.sb("seqrow", [128, 16, 128], BF16)
        sp_ = self.sb("smallp_sb", [128, 160])
        wdt = self.sb("wdt_sb", [128, KC, 16], BF16)
        a_bc = self.sb("a_bc", [128, 16])
        cwh_t = self.sb("cwh", [128, 64])
        cbh = self.sb("cbh", [128, 16])
        hnwh = self.sb("hnwh", [128, 8])
        omlh = self.sb("omlh", [128, 8])
        lbh = self.sb("lbh", [128, 8])
        lb = self.sb("lb", [128, 8])
        oml = self.sb("oml", [128, 8])
        cw = None
        cb = sp_[:, 64:80]
        dtb = sp_[:, 80:96]
        alog = sp_[:, 96:112]
        dsk = sp_[:, 112:128]
        mnw = sp_[:, 128:136]
        hnw = sp_[:, 136:144]
        lbp = sp_[:, 144:160].rearrange("p (h r) -> p h r", r=2)
        xbcT = self.sb("xbcT", [128, 16, T], BF16)
        pre = [self.sb("pre%d" % i, [128, T + 3]) for i in range(2)]
        cacc = [self.sb("cacc%d" % i, [128, T]) for i in range(2)]
        carry = self.sb("carry", [128, 16, 3])
        zs = self.sb("zs", [128, NS, D], BF16)
        vtok = self.sb("vtok", [128, NS, D], BF16)
        ar1 = self.sb("ar1", [128, 4096], BF16)
        rseg_hi = ar1[:, 0:1024]
        rseg_lo = ar1[:, 1024:2048]
        WT = ar1[:, 2048:4096].rearrange("p (h t) -> p h t", h=16)
        vm = ar1[:, :].rearrange("p (c d) -> p c d", c=4)
        ar2 = self.sb("ar2", [128, 4096], BF16)
        xs_ = ar2[:, 0:1024]
        xD = ar2[:, 1024:2048]
        xe = ar2[:, 2048:3072]
        xem = ar2[:, 3072:4096]
        scT = ar2[:, 0:1024].rearrange("p (h t) -> p h t", h=8)
        ketok = ar2[:, 1024:2048].rearrange("p (h t) -> p h t", h=8)
        osqb = ar2[:, 2048:2560]
        ar3 = self.sb("ar3", [128, 3072])
        yi = ar3[:, 0:1024]
        ysum = ar3[:, 1024:2048]
        eseg = ar3[:, 2048:3072]
        A = dict((n, ar3[:, i * T:(i + 1) * T]) for i, n in enumerate(("sig", "f", "logf", "b", "e1", "e2", "e3")))
        osq = ar3[:, 2048:3072]
        rs_ = osq
        t1 = yi
        outst = osq
        dtr = self.sb("dtr", [128, 16])
        adt = self.sb("adt", [128, 16])
        acs = self.sb("acs", [128, 16])
        ea = self.sb("ea", [128, 16])
        te = self.sb("te", [128, 16])
        dtte = self.sb("dtte", [128, 16])
        decbc = self.sb("decbc", [128, 16])
        adt_hi = self.sb("adt_hi", [128, 16], BF16)
        adt_lo = self.sb("adt_lo", [128, 16], BF16)
        adt_r = self.sb("adt_r", [128, 16])
        cmT = self.sb("cmT", [128, 4, 128], BF16)
        deccol = self.sb("deccol", [128, 8, 16])
        Btok = self.sb("Btok", [128, 4, 128], BF16)
        CBm = self.sb("CBm", [128, 4, 128])
        ssq = self.sb("ssq", [128, 4])
        gn = self.sb("gn", [128, D], BF16)
        hstT = self.sb("hstT", [128, D])
        hstT_b = self.sb("hstT_b", [128, D], BF16)
        Sst = self.sb("Sst", [128, 8, 128])
        Sst_b = self.sb("Sst_b", [128, 8, 128], BF16)
        qdT = self.sb("qdT", [128, 8, T], BF16)
        kdT = self.sb("kdT", [128, 8, T], BF16)
        keT = self.sb("keT", [128, 8, T], BF16)
        sog = self.sb("sog", [128, 8, T], BF16)
        decT = self.sb("decT", [128, 8, 32])
        sgm = self.sb("sgm", [128, 8, T], BF16)
        sgh = self.sb("sgh", [128, 8, T], BF16)
        gatedT = self.sb("gatedT", [128, 8, T], BF16)
        ofinT = self.sb("ofinT", [128, 8, T], BF16)
        mixT = self.sb("mixT", [128, 8, T], BF16)
        tmpa = self.sb("tmpa", [128, T])
        tmpb = self.sb("tmpb", [128, T])
        PA = [self.ps("pa%d" % i, [128, 512]) for i in range(6)]
        PT = [self.ps("pt%d" % i, [128, 1024], BF16) for i in range(2)]
        rot = {"M": 0, "F": 0}
        self.pt_i = 0

        def pbank(cx):
            pl = cx["pool"]
            b = PA[(0 if pl == "M" else 3) + rot[pl] % 3]
            rot[pl] += 1
            return b

        def ptbank():
            b = PT[self.pt_i % 2]
            self.pt_i += 1
            return b

        def asel(out, in_, pattern, op, base, cm):
            k.op("pool", lambda e: e.affine_select(out=out, in_=in_, pattern=pattern, compare_op=op, fill=0.0,
                                                    base=base, channel_multiplier=cm), outs=[out], ins=[in_])

        k.memset(ones_f[:], 1.0, eng="pool")
        asel(ident_f[:], ones_f[:], [[-1, 128]], ALU.is_equal, 0, 1)
        asel(tri_f[:], ones_f[:], [[1, 128]], ALU.is_ge, 0, -1)
        asel(m1_f[:], ones_f[:], [[-1, 128]], ALU.is_gt, 0, 1)
        k.copy(ident_b[:], ident_f[:], eng="pool")
        k.copy(tri_b[:], tri_f[:], eng="pool")
        k.copy(m1_b[:], m1_f[:], eng="pool")
        k.copy(ones_b[:], ones_f[:], eng="pool")
        k.memset(zero_b[:], 0.0, eng="pool")
        k.copy(bmask[:], tri_f[:], eng="pool")
        for c in range(1, 4):
            asel(bmask[:, 32 * c:32 * c + 32], bmask[:, 32 * c:32 * c + 32], [[0, 32]], ALU.is_ge, -32 * c, 1)
        for c in range(4):
            asel(cmk[:, c:c + 1], ones_f[:, 0:1], [[0, 1]], ALU.is_ge, -32 * c, 1)
            asel(cmk[:, c:c + 1], cmk[:, c:c + 1], [[0, 1]], ALU.is_gt, 32 * c + 32, -1)
        k.memset(resetm[:], 0.0, eng="pool")
        k.memset(resetm[:].rearrange("p (c j) -> p c j", j=32)[:, :, 0:1], 1.0, eng="pool")
        k.memset(resetmS[:], 0.0, eng="pool")
        k.memset(resetmS[:].rearrange("p (c j) -> p c j", j=4)[:, :, 0:1], 1.0, eng="pool")
        k.copy(triS_f[:], tri_f[:], eng="pool")
        k.copy(eseg[:, 0:128], m1_f[:], eng="pool")
        for c in range(32):
            if c > 0:
                asel(triS_f[:, 4 * c:4 * c + 4], triS_f[:, 4 * c:4 * c + 4], [[0, 4]], ALU.is_ge, -4 * c, 1)
            asel(eseg[:, 4 * c:4 * c + 4], eseg[:, 4 * c:4 * c + 4], [[0, 4]], ALU.is_gt, 4 * c + 4, -1)
        k.copy(triS_b[:], triS_f[:], eng="pool")
        k.copy(m1S_b[:], eseg[:, 0:128], eng="pool")
        k.tt(eseg[:, 0:128], eseg[:, 0:128], triS_f[:], ALU.add, eng="pool")
        k.copy(onesS_b[:], eseg[:, 0:128], eng="pool")
        asel(eseg[:, 128:144], ones_f[:, 0:16], [[-4, 16]], ALU.is_equal, -3, 1)
        k.copy(sel16[:], eseg[:, 128:144], eng="pool")
        asel(seqcol[:], ones_f[:, 0:16], [[-4, 16]], ALU.is_ge, 0, 1)
        asel(seqcol[:], seqcol[:], [[4, 16]], ALU.is_ge, 3, -1)
        k.memset(seqrow[:], 1.0, eng="pool")
        asel(seqrow[:], seqrow[:], [[-4, 16], [1, 128]], ALU.is_ge, 0, 0)
        asel(seqrow[:], seqrow[:], [[4, 16], [-1, 128]], ALU.is_ge, 3, 0)
        k.dma(sp_[:], smallp, key="c0")
        k.dma(wdt[:].rearrange("p k j -> p (k j)"), wdt_d, key="c1", eng="pool")
        k.actf(a_bc[:], alog, AF.Exp)
        k.ts(a_bc[:], a_bc[:], -1.0, None, ALU.mult)
        k.tt(lb[:], lbp[:, :, 0], lbp[:, :, 1], ALU.subtract)
        k.actf(lb[:], lb[:], AF.Tanh, scale=0.5)
        k.ts(oml[:], lb[:], -0.5, 0.5, ALU.mult, ALU.add)
        k.ts(lb[:], lb[:], 0.5, 0.5, ALU.mult, ALU.add)
        k.ts(omlh[:], oml[:], 0.5, None, ALU.mult)
        k.stt(lbh[:], oml[:], 0.5, lb[:], ALU.mult, ALU.add)
        k.ts(cwh_t[:], sp_[:, 0:64], 0.5, None, ALU.mult)
        k.ts(cbh[:], cb, 0.5, None, ALU.mult)
        k.ts(hnwh[:], hnw, 0.5, None, ALU.mult)
        cw = cwh_t[:].rearrange("p (c k) -> p c k", k=4)
        k.memset(carry[:], 0.0)
        k.memset(hstT[:], 0.0)
        k.memset(hstT_b[:], 0.0)
        k.memset(Sst[:], 0.0)
        k.memset(Sst_b[:], 0.0)

        def load_x(cx):
            xtok = xtokS[cx["st"]]
            if cx["smp"]:
                k.dma(xtok[:, 0, :], xs_d, key=("x", cx["st"], 0))
                return
            for s in range(NS):
                r0 = cx["ti"] * T + s * 128
                k.dma(xtok[:, s, :], xp[r0:r0 + 128, :], key=("x", cx["st"], s))

        def make_xT(cx):
            xtok, xT = xtokS[cx["st"]], xTS[cx["st"]]
            for s in range(cx["ns"]):
                for half in range(2):
                    pb = pbank(cx)
                    for j in range(4):
                        kc = half * 4 + j
                        k.tr(pb[:, j * 128:(j + 1) * 128], xtok[:, s, kc * 128:(kc + 1) * 128], ident_f[:])
                    k.copy(xT[:, half * 4:half * 4 + 4, s * 128:(s + 1) * 128],
                           pb[:].rearrange("p (j t) -> p j t", j=4), eng="act")
                    yield

        def ffn(cx, fi):
            tw, ns = cx["tw"], cx["ns"]
            xtok, xT = xtokS[cx["st"]], xTS[cx["st"]]
            def evac(c, pb):
                sgb = sg[c % 2]
                k.actf(sgb[:, 0:tw], pb[:, 0:tw], AF.Tanh, scale=0.5)
                k.stt(sgb[:, 0:tw], sgb[:, 0:tw], 1.0, pb[:, 0:tw], ALU.add, ALU.mult)
                k.stt(hT[:, c, 0:tw], sgb[:, 0:tw], 0.25, pb[:, tw:2 * tw], ALU.mult, ALU.mult)

            pend = None
            for c in range(NFF):
                slot = self.wload(wgu[fi], Sel(c, lambda a, c=c: a[c]), 2048, "F")
                wv = slot[:].rearrange("p (g k j) -> p g k j", g=2, k=KC)
                pb = pbank(cx)
                k.mm(pb[:, 0:tw], [(wv[:, 0, kc, :], xT[:, kc, 0:tw]) for kc in range(KC)])
                k.mm(pb[:, tw:2 * tw], [(wv[:, 1, kc, :], xT[:, kc, 0:tw]) for kc in range(KC)])
                if pend is not None:
                    evac(*pend)
                pend = (c, pb)
                yield
            evac(*pend)
            kgs = [(0, 4), (4, 8), (8, 12), (12, 16), (16, 20), (20, 22)]
            for half in range(2):
                banks = [pbank(cx) for _ in range(ns)]
                for gi, (k0, k1) in enumerate(kgs):
                    n = (k1 - k0) * 512
                    slot = self.wload(wd[fi], Sel((half, k0), lambda a, half=half, k0=k0, k1=k1: a[half][:, k0 * 512:k1 * 512]), n, "F")
                    wv = slot[:, 0:n].rearrange("p (k j) -> p k j", j=512)
                    for s in range(ns):
                        k.mm(banks[s][:], [(hT[:, kc, s * 128:(s + 1) * 128], wv[:, kc - k0, :]) for kc in range(k0, k1)],
                             start=(gi == 0), stop=(gi == len(kgs) - 1))
                for s in range(ns):
                    hs = slice(half * 512, (half + 1) * 512)
                    k.stt(xtok[:, s, hs], xtok[:, s, hs], ALPHA, banks[s][:], ALU.mult, ALU.add)
                yield

        def layer_norm(cx, s, lnb):
            x = xtokS[cx["st"]][:, s, :]
            st6, mv, rstd, nmr = lnst[cx["pool"]][s % 2]
            for hf in range(2):
                k.op("dve", lambda e, hf=hf: e.bn_stats(out=st6[:, hf, :], in_=x[:, hf * 512:(hf + 1) * 512]),
                     outs=[st6[:, hf, :]], ins=[x[:, hf * 512:(hf + 1) * 512]])
            k.op("dve", lambda e: e.bn_aggr(out=mv[:], in_=st6[:].rearrange("p a b -> p (a b)")), outs=[mv[:]], ins=[st6[:]])
            k.ts(rstd[:], mv[:, 1:2], EPS, None, ALU.add)
            k.actf(rstd[:], rstd[:], AF.Sqrt)
            k.op("dve", lambda e: e.reciprocal(out=rstd[:], in_=rstd[:]), outs=[rstd[:]], ins=[rstd[:]])
            k.stt(nmr[:], mv[:, 0:1], -1.0, rstd[:], ALU.mult, ALU.mult)
            k.actf(x, x, AF.Identity, bias=nmr[:, 0:1], scale=rstd[:, 0:1])
            k.tt(x, x, lnb[0][:], ALU.mult)
            k.tt(x, x, lnb[1][:], ALU.add)

        def load_ln(lnb, li, tag):
            q_ = "pool" if tag == "M" else "sp"
            k.dma(lnb[0][:], lnp[2 * li], key=("ln", tag, 0), eng=q_)
            k.dma(lnb[1][:], lnp[2 * li + 1], key=("ln", tag, 1), eng=q_)

        class Chunks:
            def __init__(cs, src):
                cs.src = src
                cs.i = 0
                cs.slot = None

            def next(cs):
                if cs.i % 2 == 0:
                    cs.slot = self.wload(cs.src, Sel(cs.i // 2, lambda a, u=cs.i // 2: a[u]), 2048)
                v = cs.slot[:].rearrange("p (c k j) -> p c k j", c=2, k=KC)[:, cs.i % 2]
                cs.i += 1
                return v

        def proj_feat(cx, wv):
            tw = cx["tw"]
            xT = xTS[cx["st"]]
            pb = pbank(cx)
            k.mm(pb[:, 0:tw], [(wv[:, kc, :], xT[:, kc, 0:tw]) for kc in range(KC)])
            return pb

        def bc16(ap16, lo, n, w):
            return ap16[:, lo:lo + n].unsqueeze(2).broadcast_to([128, n, w])

        def ssd_sub(cx, s):
            smp = cx["smp"]
            xT = xTS[cx["st"]]
            tok = slice(s * 128, (s + 1) * 128)
            m_tri_b = triS_b if smp else tri_b
            m_ones_b = onesS_b if smp else ones_b
            m_tri_f = triS_f if smp else tri_f
            m_m1_b = m1S_b if smp else m1_b
            pd = PA[2]
            k.mm(pd[:, 0:16], [(xT[:, kc, tok], wdt[:, kc, :]) for kc in range(KC)])
            k.tt(dtr[:], pd[:, 0:16], dtb, ALU.add)
            k.actf(dtr[:], dtr[:], AF.Exp)
            k.actf(dtr[:], dtr[:], AF.Ln, bias=1.0)
            k.tt(adt[:], dtr[:], a_bc[:], ALU.mult)
            k.copy(adt_hi[:], adt[:])
            k.tt(adt_r[:], adt[:], adt_hi[:], ALU.subtract)
            k.copy(adt_lo[:], adt_r[:])
            pc = PA[1]
            k.mm(pc[:, 0:16], [(m_tri_b[:], adt_hi[:]), (m_tri_b[:], adt_lo[:])])
            k.mm(pc[:, 16:32], [(m_ones_b[:], adt_hi[:]), (m_ones_b[:], adt_lo[:])])
            k.actf(ea[:], pc[:, 0:16], AF.Exp)
            k.actf(decbc[:], pc[:, 16:32], AF.Exp)
            k.copy(acs[:], pc[:, 0:16], eng="act")
            k.tt(te[:], pc[:, 16:32], acs[:], ALU.subtract)
            k.actf(te[:], te[:], AF.Exp)
            k.tt(dtte[:], dtr[:], te[:], ALU.mult)
            yield
            pcb = PA[2]
            for g in range(4):
                k.mm(pcb[:, g * 128:(g + 1) * 128], [(xbcT[:, 8 + g, tok], xbcT[:, 12 + g, tok])])
            k.tt(CBm[:], pcb[:].rearrange("p (g t) -> p g t", g=4), m_tri_f[:].unsqueeze(1).broadcast_to([128, 4, 128]), ALU.mult)
            for hf in range(2):
                k.tt(rseg_hi.rearrange("p (h t) -> p h t", h=8), bc16(adt_hi, hf * 8, 8, 128),
                     m_tri_f[:].unsqueeze(1).broadcast_to([128, 8, 128]), ALU.mult)
                k.tt(rseg_lo.rearrange("p (h t) -> p h t", h=8), bc16(adt_lo, hf * 8, 8, 128),
                     m_tri_f[:].unsqueeze(1).broadcast_to([128, 8, 128]), ALU.mult)
                for q in range(2):
                    pq = PA[q]
                    k.mm(pq[:], [(m_m1_b[:], rseg_hi[:, q * 512:(q + 1) * 512]), (m_m1_b[:], rseg_lo[:, q * 512:(q + 1) * 512])])
                    k.actf(eseg[:, q * 512:(q + 1) * 512], pq[:], AF.Exp)
                k.tt(WT[:, hf * 8:(hf + 1) * 8, :].rearrange("p (g r) t -> p g r t", g=2),
                     eseg.rearrange("p (g r t) -> p g r t", g=2, r=4),
                     CBm[:, hf * 2:hf * 2 + 2, :].unsqueeze(2).broadcast_to([128, 2, 4, 128]), ALU.mult)
                yield
            ptx = ptbank()
            for j in range(8):
                k.tr(ptx[:, j * 128:(j + 1) * 128], xbcT[:, j, tok], ident_b[:])
            ptb = ptbank()
            for g in range(4):
                k.tr(ptb[:, g * 128:(g + 1) * 128], xbcT[:, 8 + g, tok], ident_b[:])
            xv = ptx[:].rearrange("p (h j) -> p h j", h=16)
            k.tt(xs_.rearrange("p (h j) -> p h j", h=16), xv, bc16(dtr, 0, 16, 64), ALU.mult)
            k.tt(xD.rearrange("p (h j) -> p h j", h=16), xv, bc16(dsk, 0, 16, 64), ALU.mult)
            k.tt(xe.rearrange("p (h j) -> p h j", h=16), xv, bc16(dtte, 0, 16, 64), ALU.mult)
            k.copy(Btok[:].rearrange("p g n -> p (g n)"), ptb[:, 0:512], eng="act")
            yield
            for hf in range(2):
                py = PA[hf]
                k.mm(py[:], [(ident_b[:], xD[:, hf * 512:(hf + 1) * 512])], start=True, stop=False, skip=True)
                for hh in range(8):
                    h = hf * 8 + hh
                    k.mm(py[:, hh * 64:(hh + 1) * 64], [(WT[:, h, :], xs_[:, h * 64:(h + 1) * 64])], start=False, stop=(hh == 7),
                         skip=True)
                k.copy(ysum[:, hf * 512:(hf + 1) * 512], py[:], eng="act")
            yield
            pyi = [PA[0], PA[1]]
            if not smp:
                for g in range(4):
                    k.mm(pyi[g // 2][:, (g % 2) * 256:(g % 2 + 1) * 256], [(xbcT[:, 12 + g, tok], hstT_b[:, g * 256:(g + 1) * 256])])
            else:
                k.copy(adt_hi[:], decbc[:])
                k.tt(adt_r[:], decbc[:], adt_hi[:], ALU.subtract)
                k.copy(adt_lo[:], adt_r[:])
                k.copy(rseg_hi.rearrange("p (h j) -> p h j", h=16), bc16(adt_hi, 0, 16, 64))
                k.copy(rseg_lo.rearrange("p (h j) -> p h j", h=16), bc16(adt_lo, 0, 16, 64))
                pdc = PA[2]
                for j in range(8):
                    k.mm(pdc[:, j * 16:(j + 1) * 16], [(rseg_hi[:, j * 128:(j + 1) * 128], sel16[:]),
                                                        (rseg_lo[:, j * 128:(j + 1) * 128], sel16[:])])
                k.copy(deccol[:].rearrange("p j q -> p (j q)"), pdc[:, 0:128], eng="act")
                for hb in range(2):
                    k.mm(pyi[hb][:], [(zero_b[:], xs_[:, 0:512])], start=True, stop=False, skip=True)
                hbufs = [hstT, xtokS[cx["st"]][:, 1, :]]
                for q in range(16):
                    hq = hbufs[q % 2]
                    k.dma(hq.rearrange("p (j n) -> p j n", j=8) if q % 2 else hstT[:].rearrange("p (j n) -> p j n", j=8),
                          sssm_d[q].rearrange("(j p) n -> p j n", p=128), key=("ssin", q % 2), eng="act")
                    for half in range(2):
                        pb = PA[2]
                        for jj in range(4):
                            j = half * 4 + jj
                            k.tr(pb[:, jj * 128:(jj + 1) * 128], hq[:, j * 128:(j + 1) * 128], ident_f[:])
                        k.copy(hstT_b[:, half * 512:(half + 1) * 512], pb[:], eng="act")
                    k.tt(cmT[:], xbcT[:, 12:16, tok], seqrow[:, q, :].unsqueeze(1).broadcast_to([128, 4, 128]), ALU.mult)
                    for g in range(4):
                        k.mm(pyi[g // 2][:, (g % 2) * 256:(g % 2 + 1) * 256], [(cmT[:, g, :], hstT_b[:, g * 256:(g + 1) * 256])],
                             start=False, stop=(q == 15 and g % 2 == 1), skip=True)
                    k.actf(xem, xe, AF.Copy, scale=seqcol[:, q:q + 1])
                    for half in range(2):
                        pst = PA[2]
                        for jj in range(4):
                            j = half * 4 + jj
                            k.mm(pst[:, jj * 128:(jj + 1) * 128], [(xem[:, j * 128:(j + 1) * 128], Btok[:, j // 2, :])])
                        for jj in range(4):
                            j = half * 4 + jj
                            k.stt(osq[:, j * 128:(j + 1) * 128], hq[:, j * 128:(j + 1) * 128], deccol[:, j, q:q + 1],
                                  pst[:, jj * 128:(jj + 1) * 128], ALU.mult, ALU.add)
                    k.dma(ssm_s[q].rearrange("(j p) n -> p j n", p=128), osq.rearrange("p (j n) -> p j n", j=8),
                          key="ssout", eng="act", sb_out=False)
                    yield
            for hf in range(2):
                hs = slice(hf * 512, (hf + 1) * 512)
                k.tt(yi[:, hs].rearrange("p (h j) -> p h j", h=8), pyi[hf][:].rearrange("p (h j) -> p h j", h=8),
                     bc16(ea, hf * 8, 8, 64), ALU.mult)
                k.tt(ysum[:, hs], ysum[:, hs], yi[:, hs], ALU.add)
            yield
            if not smp:
                for hf in range(2):
                    pst = PA[2]
                    for gg in range(2):
                        g = hf * 2 + gg
                        k.mm(pst[:, gg * 256:(gg + 1) * 256], [(Btok[:, g, :], xe[:, g * 256:(g + 1) * 256])])
                    hs = slice(hf * 512, (hf + 1) * 512)
                    k.tt(hstT[:, hs].rearrange("p (h j) -> p h j", h=8), hstT[:, hs].rearrange("p (h j) -> p h j", h=8),
                         bc16(decbc, hf * 8, 8, 64), ALU.mult)
                    k.tt(hstT[:, hs], hstT[:, hs], pst[:], ALU.add)
                    k.copy(hstT_b[:, hs], hstT[:, hs], eng="act")
                yield
            k.stt(ysum, ysum, 0.5, zs[:, s, :], ALU.mult, ALU.mult)
            k.tt(yi, ysum, ysum, ALU.mult)
            k.op("dve", lambda e: e.tensor_reduce(out=ssq[:], in_=yi.rearrange("p (g j) -> p g j", g=4),
                                                   axis=mybir.AxisListType.X, op=ALU.add),
                 outs=[ssq[:]], ins=[yi])
            k.ts(ssq[:], ssq[:], 1.0 / 256.0, EPS, ALU.mult, ALU.add)
            k.actf(ssq[:], ssq[:], AF.Sqrt)
            k.op("dve", lambda e: e.reciprocal(out=ssq[:], in_=ssq[:]), outs=[ssq[:]], ins=[ssq[:]])
            k.tt(gn[:].rearrange("p (g j) -> p g j", g=4), ysum.rearrange("p (g j) -> p g j", g=4),
                 ssq[:].unsqueeze(2).broadcast_to([128, 4, 256]), ALU.mult)
            ptg = ptbank()
            for j in range(8):
                k.tr(ptg[:, j * 128:(j + 1) * 128], gn[:, j * 128:(j + 1) * 128], ident_b[:])
            k.tt(gatedT[:, :, tok], ptg[:].rearrange("p (j t) -> p j t", j=8), mnw.unsqueeze(2).broadcast_to([128, 8, 128]), ALU.mult)
            yield

        def hgrn_prep(cx, h, pf, pq, pog):
            tw, cl = cx["tw"], cx["cl"]
            nch = tw // cl
            rm = resetmS if cx["smp"] else resetm
            a = dict((n, v[:, 0:tw]) for n, v in A.items())
            k.actf(a["sig"], pf[:, 0:tw], AF.Tanh, scale=0.5)
            k.ts(a["f"], a["sig"], omlh[:, h:h + 1], lbh[:, h:h + 1], ALU.mult, ALU.add)
            k.op("dve", lambda e: e.tensor_tensor_scan(out=a["b"], data0=rm[:, 0:tw], data1=a["f"],
                                                        initial=1.0, op0=ALU.max, op1=ALU.mult),
                 outs=[a["b"]], ins=[rm[:, 0:tw], a["f"]])
            k.ts(a["f"], a["f"], -1.0, 1.0, ALU.mult, ALU.add)
            k.op("dve", lambda e: e.reciprocal(out=a["e2"], in_=a["b"]), outs=[a["e2"]], ins=[a["b"]])
            bv = a["b"].rearrange("p (c j) -> p c j", j=cl)
            k.tt(a["e3"].rearrange("p (c j) -> p c j", j=cl), bv[:, :, cl - 1:cl].broadcast_to([128, nch, cl]),
                 a["e2"].rearrange("p (c j) -> p c j", j=cl), ALU.mult)
            k.copy(decT[:, h, 0:nch], bv[:, :, cl - 1], eng="act")
            k.tt(qdT[:, h, 0:tw], pq[:, 0:tw], a["b"], ALU.mult)
            k.tt(kdT[:, h, 0:tw], a["f"], a["e2"], ALU.mult)
            k.tt(keT[:, h, 0:tw], a["f"], a["e3"], ALU.mult)
            k.actf(a["e1"], pog[:, 0:tw], AF.Tanh, scale=0.5)
            k.stt(sog[:, h, 0:tw], a["e1"], 1.0, pog[:, 0:tw], ALU.add, ALU.mult)

        def hgrn_sub(cx, s):
            smp = cx["smp"]
            tok = slice(s * 128, (s + 1) * 128)
            msk = triS_f if smp else bmask
            psc = [PA[1], PA[2]]
            for h in range(8):
                k.mm(psc[h // 4][:, (h % 4) * 128:(h % 4 + 1) * 128], [(kdT[:, h, tok], qdT[:, h, tok])])
            for hb in range(2):
                k.tt(scT[:, hb * 4:hb * 4 + 4, :], psc[hb][:].rearrange("p (h t) -> p h t", h=4),
                     msk[:].unsqueeze(1).broadcast_to([128, 4, 128]), ALU.mult)
            ptk = ptbank()
            for h in range(8):
                k.tr(ptk[:, h * 128:(h + 1) * 128], keT[:, h, tok], ident_b[:])
            k.copy(ketok.rearrange("p h k -> p (h k)"), ptk[:], eng="act")
            if not smp:
                for c in range(4):
                    k.actf(vm[:, c, :], vtok[:, s, :], AF.Copy, scale=cmk[:, c:c + 1])
            yield
            po = [PA[0], PA[1]]
            for hb in range(2):
                k.mm(po[hb][:], [(zero_b[:], gn[:, 0:512])], start=True, stop=False, skip=True)
            for h in range(8):
                k.mm(po[h // 4][:, (h % 4) * 128:(h % 4 + 1) * 128], [(vtok[:, s, h * 128:(h + 1) * 128], scT[:, h, :])],
                     start=False, stop=False, skip=True)
            pSs = [PA[2], PA[2]]
            if not smp:
                for c in range(4):
                    ci = s * 4 + c
                    for h in range(8):
                        col = (h % 4) * 128 + 32 * c
                        k.mm(po[h // 4][:, col:col + 32], [(Sst_b[:, h, :], qdT[:, h, s * 128 + 32 * c:s * 128 + 32 * c + 32])],
                             start=False, stop=(c == 3 and h % 4 == 3), skip=True)
                        pS = pSs[h // 4]
                        k.mm(pS[:, (h % 4) * 128:(h % 4 + 1) * 128], [(ketok[:, h, :], vm[:, c, h * 128:(h + 1) * 128])])
                        if h % 4 == 3:
                            for h2 in range(h - 3, h + 1):
                                k.stt(Sst[:, h2, :], Sst[:, h2, :], decT[:, h2, ci:ci + 1], pS[:, (h2 % 4) * 128:(h2 % 4 + 1) * 128],
                                      ALU.mult, ALU.add)
                            k.copy(Sst_b[:, h - 3:h + 1, :], Sst[:, h - 3:h + 1, :], eng="act")
                    yield
            else:
                for q in range(16):
                    k.dma(Sst[:], shg_d[q].rearrange("(h k) v -> k h v", k=128), key="hgin", eng="act")
                    k.copy(Sst_b[:].rearrange("p h v -> p (h v)"), Sst[:].rearrange("p h v -> p (h v)"), eng="act")
                    k.actf(vm[:, 0, :], vtok[:, 0, :], AF.Copy, scale=seqcol[:, q:q + 1])
                    for h in range(8):
                        col = (h % 4) * 128 + 4 * q
                        k.mm(po[h // 4][:, col:col + 4], [(Sst_b[:, h, :], qdT[:, h, 4 * q:4 * q + 4])], start=False,
                             stop=(q == 15 and h % 4 == 3), skip=True)
                        pS = pSs[h // 4]
                        k.mm(pS[:, (h % 4) * 128:(h % 4 + 1) * 128], [(ketok[:, h, :], vm[:, 0, h * 128:(h + 1) * 128])])
                        if h % 4 == 3:
                            for h2 in range(h - 3, h + 1):
                                k.stt(Sst[:, h2, :], Sst[:, h2, :], decT[:, h2, q:q + 1], pS[:, (h2 % 4) * 128:(h2 % 4 + 1) * 128],
                                      ALU.mult, ALU.add)
                    k.dma(hg_s[q].rearrange("(h k) v -> k h v", k=128), Sst[:], key="hgout", eng="act", sb_out=False)
                    yield
            for hb in range(2):
                hs = slice(hb * 512, (hb + 1) * 512)
                k.actf(osqb, po[hb][:], AF.Square)
                pss = PA[2]
                k.mm(pss[:], [(ones_b[:], osqb)])
                k.ts(rs_[:, hs], pss[:], 1.0 / 128.0, EPS, ALU.mult, ALU.add)
                k.actf(rs_[:, hs], rs_[:, hs], AF.Sqrt)
                k.op("dve", lambda e, hs=hs: e.reciprocal(out=rs_[:, hs], in_=rs_[:, hs]), outs=[rs_[:, hs]], ins=[rs_[:, hs]])
                k.tt(t1[:, hs], po[hb][:], rs_[:, hs], ALU.mult)
            for h in range(8):
                k.stt(ofinT[:, h, tok], t1[:, h * 128:(h + 1) * 128], hnwh[:, h:h + 1], sog[:, h, tok], ALU.mult, ALU.mult)
            yield

        def conv_A(cx, c, pp):
            tw, smp = cx["tw"], cx["smp"]
            pr = pre[c % 2]
            if not smp:
                k.copy(pr[:, 0:3], carry[:, c, :])
                k.copy(pr[:, 3:3 + tw], pp[:, 0:tw], eng="act")
                k.copy(carry[:, c, :], pr[:, tw:tw + 3])
                return
            pr3 = pr[:, 0:112].rearrange("p (q j) -> p q j", j=7)
            src = ysum if c < 8 else yi
            ph = pbank(cx)
            k.tr(ph[:, 0:128], src[:, (c % 8) * 128:(c % 8 + 1) * 128], ident_f[:])
            k.copy(pr3[:, :, 0:3], ph[:, 0:48].rearrange("p (q j) -> p q j", j=3))
            k.copy(pr3[:, :, 3:7], pp[:, 0:64].rearrange("p (q j) -> p q j", j=4), eng="act")
            k.copy(tmpa[:, 0:128], pp[:, 0:128], eng="act")
            pt2 = pbank(cx)
            k.tr(pt2[:, 0:128], tmpa[:, 0:128], ident_f[:])
            stg = hstT if c < 8 else Sst[:].rearrange("p h v -> p (h v)")
            k.copy(stg[:, (c % 8) * 128:(c % 8 + 1) * 128], pt2[:, 0:128])

        def conv_B(cx, c):
            tw, smp = cx["tw"], cx["smp"]
            pr = pre[c % 2]
            ca = cacc[c % 2]
            if not smp:
                k.ts(ca[:, 0:tw], pr[:, 0:tw], cw[:, c, 0:1], cbh[:, c:c + 1], ALU.mult, ALU.add)
                for kk in range(1, 4):
                    k.stt(ca[:, 0:tw], pr[:, kk:kk + tw], cw[:, c, kk:kk + 1], ca[:, 0:tw], ALU.mult, ALU.add)
                k.actf(pr[:, 0:tw], ca[:, 0:tw], AF.Tanh)
                k.stt(xbcT[:, c, 0:tw], pr[:, 0:tw], 1.0, ca[:, 0:tw], ALU.add, ALU.mult)
                return
            pr3 = pr[:, 0:112].rearrange("p (q j) -> p q j", j=7)
            ca3 = ca[:, 0:64].rearrange("p (q j) -> p q j", j=4)
            k.ts(ca3, pr3[:, :, 0:4], cw[:, c, 0:1], cbh[:, c:c + 1], ALU.mult, ALU.add)
            for kk in range(1, 4):
                k.stt(ca3, pr3[:, :, kk:kk + 4], cw[:, c, kk:kk + 1], ca3, ALU.mult, ALU.add)
            k.actf(pr[:, 112:176], ca[:, 0:64], AF.Tanh)
            k.stt(xbcT[:, c, 0:64], pr[:, 112:176], 1.0, ca[:, 0:64], ALU.add, ALU.mult)

        def mixer(cx):
            tw, ns, smp = cx["tw"], cx["ns"], cx["smp"]
            xtok, xT = xtokS[cx["st"]], xTS[cx["st"]]
            for g in range(4):
                banks = [pbank(cx) for _ in range(ns)]
                for kh in range(2):
                    slot = self.wload(wtok, Sel((g, kh), lambda a, g=g, kh=kh: a[g][:, kh * 2048:(kh + 1) * 2048]), 2048)
                    wv = slot[:].rearrange("p (k j) -> p k j", j=512)
                    for s in range(ns):
                        k.mm(banks[s][:], [(xT[:, kh * 4 + kc, s * 128:(s + 1) * 128], wv[:, kc, :]) for kc in range(4)],
                             start=(kh == 0), stop=(kh == 1))
                hs = slice((g % 2) * 512, (g % 2 + 1) * 512)
                for s in range(ns):
                    if g < 2:
                        k.actf(yi[:, 0:512], banks[s][:], AF.Tanh, scale=0.5)
                        k.stt(zs[:, s, hs], yi[:, 0:512], 1.0, banks[s][:], ALU.add, ALU.mult)
                    else:
                        k.copy(vtok[:, s, hs], banks[s][:])
                yield
            cs = Chunks(winf)
            if smp:
                k.dma(ysum, sconv_d[:, 0:1024], key="scv0", eng="act")
                k.dma(yi, sconv_d[:, 1024:2048], key="scv1", eng="act")
                k.memset(xbcT[:, :, 64:128], 0.0)
            for c in range(16):
                conv_A(cx, c, proj_feat(cx, cs.next()))
                if c > 0:
                    conv_B(cx, c - 1)
                yield
            conv_B(cx, 15)
            if smp:
                cv = conv_s.rearrange("(q j) c -> q j c", j=3)
                for j in range(1, 4):
                    for half in range(2):
                        stg = hstT if half == 0 else Sst[:].rearrange("p h v -> p (h v)")
                        k.dma(cv[:, j - 1, half * 1024:(half + 1) * 1024], stg[j:64:4, :], key=("cvo", half, j), eng="act",
                              sb_out=False)
            for s in range(ns):
                yield from ssd_sub(cx, s)
            for h in range(8):
                pf = proj_feat(cx, cs.next())
                pq = proj_feat(cx, cs.next())
                pog = proj_feat(cx, cs.next())
                hgrn_prep(cx, h, pf, pq, pog)
                yield
            for s in range(ns):
                yield from hgrn_sub(cx, s)
            for j in range(8):
                pg = proj_feat(cx, cs.next())
                k.actf(sgm[:, j, 0:tw], pg[:, 0:tw], AF.Tanh, scale=0.5)
                k.actf(sgm[:, j, 0:tw], sgm[:, j, 0:tw], AF.Identity, bias=0.5, scale=0.5)
                yield
            for j in range(8):
                pg = proj_feat(cx, cs.next())
                k.actf(sgh[:, j, 0:tw], pg[:, 0:tw], AF.Tanh, scale=0.5)
                k.actf(sgh[:, j, 0:tw], sgh[:, j, 0:tw], AF.Identity, bias=0.5, scale=0.5)
                yield
            cm_ = Chunks(wmo)
            ch_ = Chunks(who)
            for j in range(8):
                wm = cm_.next()
                wh = ch_.next()
                pm = pbank(cx)
                ph = pbank(cx)
                k.mm(pm[:, 0:tw], [(wm[:, kc, :], gatedT[:, kc, 0:tw]) for kc in range(KC)])
                k.mm(ph[:, 0:tw], [(wh[:, kc, :], ofinT[:, kc, 0:tw]) for kc in range(KC)])
                k.tt(tmpa[:, 0:tw], pm[:, 0:tw], sgm[:, j, 0:tw], ALU.mult)
                k.tt(tmpb[:, 0:tw], ph[:, 0:tw], sgh[:, j, 0:tw], ALU.mult)
                k.tt(mixT[:, j, 0:tw], tmpa[:, 0:tw], tmpb[:, 0:tw], ALU.add)
                yield
            for half in range(2):
                banks = [pbank(cx) for _ in range(ns)]
                for kh in range(2):
                    slot = self.wload(wo, Sel((half, kh), lambda a, half=half, kh=kh: a[half][:, kh * 2048:(kh + 1) * 2048]), 2048)
                    wv = slot[:].rearrange("p (k j) -> p k j", j=512)
                    for s in range(ns):
                        k.mm(banks[s][:], [(mixT[:, kh * 4 + kc, s * 128:(s + 1) * 128], wv[:, kc, :]) for kc in range(4)],
                             start=(kh == 0), stop=(kh == 1))
                hs = slice(half * 512, (half + 1) * 512)
                for s in range(ns):
                    k.stt(xtok[:, s, hs], xtok[:, s, hs], ALPHA, banks[s][:], ALU.mult, ALU.add)
                yield

        def store_y(cx):
            xtok = xtokS[cx["st"]]
            if cx["smp"]:
                k.dma(y_s[:, :], xtok[0:64, 0, :], key=("y", cx["st"], 0), sb_out=False)
                return
            for s in range(NS):
                r0 = cx["ti"] * T + s * 128
                k.dma(y_p[r0:r0 + 128, :], xtok[:, s, :], key=("y", cx["st"], s), sb_out=False)

        def final_prompt_states(cx):
            for q in range(4):
                pb = pbank(cx)
                for j in range(4):
                    c = q * 4 + j
                    k.tr(pb[0:3, j * 128:(j + 1) * 128], carry[:, c, :], ident_f[:])
                k.copy(osq[0:3, 0:512], pb[0:3, :], eng="act")
                k.dma(conv_p[:, q * 512:(q + 1) * 512], osq[0:3, 0:512], key="oc", eng="pool", sb_out=False)
            for q in range(2):
                pb = pbank(cx)
                for j in range(4):
                    c = q * 4 + j
                    k.tr(pb[:, j * 128:(j + 1) * 128], hstT[:, c * 128:(c + 1) * 128], ident_f[:])
                k.copy(outst[:, q * 512:(q + 1) * 512], pb[:], eng="act")
            k.dma(ssm_p.rearrange("(c p) n -> p c n", p=128), outst.rearrange("p (c n) -> p c n", n=128), key="os", eng="pool", sb_out=False)
            k.dma(hg_p.rearrange("(h k) v -> k h v", k=128), Sst[:], key="oh", eng="pool", sb_out=False)

        tiles = [dict(ti=t, ns=NS, tw=T, smp=False, cl=32, st=t % 2) for t in range(NTILE)]
        tiles.append(dict(ti=NTILE, ns=1, tw=128, smp=True, cl=4, st=NTILE % 2))
        if mode == "sample_only":
            tiles = [dict(ti=0, ns=1, tw=128, smp=True, cl=4, st=0)]
        NTT = len(tiles)

        def F1(t):
            cx = dict(tiles[t], pool="F")
            load_x(cx)
            yield from make_xT(cx)
            load_ln(lnF, 0, "F")
            yield from ffn(cx, 0)
            for s in range(cx["ns"]):
                layer_norm(cx, s, lnF)
                yield
            yield from make_xT(cx)

        def Mx(t):
            cx = dict(tiles[t], pool="M")
            load_ln(lnM, 1, "M")
            yield from mixer(cx)
            for s in range(cx["ns"]):
                layer_norm(cx, s, lnM)
                yield
            yield from make_xT(cx)
            if not cx["smp"] and t + 1 < NTT and tiles[t + 1]["smp"]:
                final_prompt_states(cx)
                yield

        def F2(t):
            cx = dict(tiles[t], pool="F")
            load_ln(lnF, 2, "F")
            yield from ffn(cx, 1)
            for s in range(cx["ns"]):
                layer_norm(cx, s, lnF)
                yield
            store_y(cx)
            yield

        def chain(*gens):
            for g in gens:
                yield from g

        def run_pair(ga, gb):
            la = list_len.get(ga[0], 1)
            lb_ = list_len.get(gb[0], 1)
            ia = ib = 0
            a, b = ga[1], gb[1]
            da = db = False
            while not (da and db):
                if not da and (db or ia * lb_ <= ib * la):
                    try:
                        next(a)
                        ia += 1
                    except StopIteration:
                        da = True
                elif not db:
                    try:
                        next(b)
                        ib += 1
                    except StopIteration:
                        db = True

        list_len = {"M": 110, "Ms": 150, "F": 150, "F1": 75, "F2": 75}

        for _ in F1(0):
            pass
        for kslot in range(NTT + 1):
            m_gen = Mx(kslot) if kslot < NTT else None
            f_parts = []
            if kslot >= 1:
                f_parts.append(F2(kslot - 1))
            if kslot + 1 < NTT:
                f_parts.append(F1(kslot + 1))
            f_gen = chain(*f_parts) if f_parts else None
            if m_gen is not None and f_gen is not None:
                mk = "Ms" if tiles[kslot]["smp"] else "M"
                fk = "F" if len(f_parts) == 2 else "F1"
                run_pair((mk, m_gen), (fk, f_gen))
            elif m_gen is not None:
                for _ in m_gen:
                    pass
            elif f_gen is not None:
                for _ in f_gen:
                    pass

        k.emit()
        return nc


def _wgu_layout(wg, wu):
    g = wg.reshape(KC, 128, NFF, 128)
    u = wu.reshape(KC, 128, NFF, 128)
    a = np.stack([g, u], axis=0)
    a = a.transpose(3, 2, 0, 1, 4)
    return np.ascontiguousarray(a).reshape(NFF, 128, 2048)


def _wd_layout(wd):
    a = wd.reshape(NFF, 128, 2, 512).transpose(2, 1, 0, 3)
    return np.ascontiguousarray(a).reshape(2, 128, NFF * 512)


def _chunks_layout(w):
    n = w.shape[1] // 128
    return w.reshape(KC, 128, n, 128).transpose(2, 1, 0, 3)


def _pack4(ch):
    n = ch.shape[0]
    a = ch.reshape(n // 2, 2, 128, KC, 128).transpose(0, 2, 1, 3, 4)
    return np.ascontiguousarray(a).reshape(n // 2, 128, 2048)


def _half_layout(w):
    a = w.reshape(KC, 128, 2, 512).transpose(2, 1, 0, 3)
    return np.ascontiguousarray(a).reshape(2, 128, 4096)


_CACHE = {}


def _get_nc(mode):
    if mode not in _CACHE:
        _CACHE[mode] = Builder(mode).build()
    return _CACHE[mode]


def kernel(**inp):
    mode = inp.pop("_mode", "full")
    f = lambda a: np.ascontiguousarray(np.asarray(a, dtype=np.float32))
    x_prompt = f(inp["x_prompt"])
    w_in = f(inp["w_in"])[0]
    o_z, o_xbc, o_dt, o_q, o_f, o_i, o_og, o_gm, o_gh = 0, 1024, 3072, 3088, 4112, 5136, 6160, 7184, 8208
    fchunks = [_chunks_layout(w_in[:, o_xbc:o_xbc + 2048])]
    cf = _chunks_layout(w_in[:, o_f:o_f + 1024])
    cq = _chunks_layout(w_in[:, o_q:o_q + 1024])
    cog = _chunks_layout(w_in[:, o_og:o_og + 1024])
    for h in range(8):
        fchunks += [cf[h:h + 1], cq[h:h + 1], cog[h:h + 1]]
    fchunks += [_chunks_layout(w_in[:, o_gm:o_gm + 1024]), _chunks_layout(w_in[:, o_gh:o_gh + 1024])]
    winf = _pack4(np.concatenate(fchunks, axis=0))
    wtok = np.concatenate([_half_layout(w_in[:, o_z:o_z + 1024]), _half_layout(w_in[:, o_i:o_i + 1024])], axis=0)
    wdt = np.ascontiguousarray(w_in[:, o_dt:o_dt + 16].reshape(KC, 128, 16).transpose(1, 0, 2)).reshape(128, 128)
    smallp = np.zeros((128, 160), np.float32)
    smallp[:, 0:64] = f(inp["conv_w"])[0].reshape(4, 16, 128).transpose(2, 1, 0).reshape(128, 64)
    smallp[:, 64:80] = f(inp["conv_b"])[0].reshape(16, 128).T
    smallp[:, 80:96] = f(inp["dt_bias"])[0][None, :]
    smallp[:, 96:112] = f(inp["a_log"])[0][None, :]
    smallp[:, 112:128] = f(inp["d_skip"])[0][None, :]
    smallp[:, 128:136] = f(inp["m_norm_w"])[0].reshape(8, 128).T
    smallp[:, 136:144] = f(inp["h_norm_w"])[0].reshape(8, 128).T
    smallp[:, 144:160] = f(inp["hgrn_lb_param"]).reshape(2, 8, 128).transpose(2, 1, 0).reshape(128, 16)
    shared = {
        "wgu1": _wgu_layout(f(inp["ffn1_w_gate"])[0], f(inp["ffn1_w_up"])[0]),
        "wd1": _wd_layout(f(inp["ffn1_w_down"])[0]),
        "wgu2": _wgu_layout(f(inp["ffn2_w_gate"])[0], f(inp["ffn2_w_up"])[0]),
        "wd2": _wd_layout(f(inp["ffn2_w_down"])[0]),
        "lnp": np.ascontiguousarray(np.broadcast_to(
            np.stack([f(inp[n])[0] for n in ("ln1_g", "ln1_b", "ln2_g", "ln2_b", "ln3_g", "ln3_b")])[:, None, :],
            (6, 128, D))),
        "wtok": wtok, "wdt": wdt, "winf": winf,
        "wmo": _pack4(_chunks_layout(f(inp["w_m_out"])[0])),
        "who": _pack4(_chunks_layout(f(inp["w_h_out"])[0])),
        "wo": _half_layout(f(inp["w_o"])[0]),
        "smallp": smallp,
    }
    x_sample = f(inp["x_sample"])
    state_conv = f(inp["state_conv"])[0]
    state_ssm = f(inp["state_ssm"])[0]
    state_hgrn = f(inp["state_hgrn"])[0]
    in_maps = []
    for c in range(NCORES):
        m = dict(shared)
        m["xp"] = x_prompt[c]
        xs = np.zeros((128, D), np.float32)
        xs[0:64] = x_sample[16 * c:16 * c + 16].reshape(64, D)
        m["xs"] = xs
        sc = np.zeros((128, 2048), np.float32)
        sc[0:48] = state_conv[16 * c:16 * c + 16].reshape(48, 2048)
        m["sconv"] = sc
        m["sssm"] = np.ascontiguousarray(state_ssm[16 * c:16 * c + 16].reshape(16, 1024, 128))
        m["shg"] = np.ascontiguousarray(state_hgrn[16 * c:16 * c + 16].reshape(16, 1024, 128))
        in_maps.append(m)
    nc = _get_nc(mode)
    res = run_bass_kernel_spmd(nc, in_maps, core_ids=list(range(NCORES)))
    r = res.results
    y_p = np.stack([r[c]["y_p"] for c in range(NCORES)])
    conv_p = np.stack([r[c]["conv_p"] for c in range(NCORES)])[None]
    ssm_p = np.stack([r[c]["ssm_p"] for c in range(NCORES)]).reshape(1, NCORES, 16, 64, 128)
    hg_p = np.stack([r[c]["hg_p"] for c in range(NCORES)]).reshape(1, NCORES, 8, 128, 128)
    y_s = np.concatenate([r[c]["y_s"] for c in range(NCORES)]).reshape(128, 4, D)
    conv_s = np.concatenate([r[c]["conv_s"] for c in range(NCORES)]).reshape(1, 128, 3, 2048)
    ssm_s = np.concatenate([r[c]["ssm_s"] for c in range(NCORES)]).reshape(1, 128, 16, 64, 128)
    hg_s = np.concatenate([r[c]["hg_s"] for c in range(NCORES)]).reshape(1, 128, 8, 128, 128)
    return (y_p, y_s, conv_p, ssm_p, hg_p, conv_s, ssm_s, hg_s)
```
